# Optimizing a Trainium2 kernel written in Bass

```python
import jax
import jax.numpy as jnp
from jax import lax
import numpy as np

D_MODEL = 2048
BATCH = 32
SEQ = 256
DEPTH = 4
DEC_BATCH = 4
DEC_SEQ = 2048
PAST_LEN = 512

GRID_W = 64
N_MIXERS = 3
N_POOL_LAYERS = (DEPTH + 2) // 3
N_SSD_LAYERS = (DEPTH + 1) // 3
N_CONV_LAYERS = DEPTH // 3
N_MOD = 9
D_FF = 5632
EPS = 1e-6
POOL_WINDOWS = (2, 4, 8, 16)
N_POOL_GROUPS = 4
POOL_GROUP_DIM = D_MODEL // N_POOL_GROUPS
SSD_D_INNER = 2 * D_MODEL
SSD_HEAD_DIM = 64
SSD_HEADS = SSD_D_INNER // SSD_HEAD_DIM
SSD_GROUPS = 8
SSD_STATE = 128
SSD_CONV = 5
SSD_CHUNK = 128
SSD_BC_DIM = SSD_GROUPS * SSD_STATE
SSD_CONV_DIM = SSD_D_INNER + 2 * SSD_BC_DIM
SSD_IN_DIM = SSD_D_INNER + SSD_CONV_DIM + 2 * SSD_HEADS
CM_KERNEL = 31

kernel_name = "hybrid_pool_ssd_conformer_diffusion_step"


def _rms(x, w):
    xf = x.astype(jnp.float32)
    y = xf * lax.rsqrt(jnp.mean(xf * xf, axis=-1, keepdims=True) + EPS)
    return (y * w.astype(jnp.float32)).astype(x.dtype)


def _layernorm(x, w, b):
    xf = x.astype(jnp.float32)
    mu = jnp.mean(xf, axis=-1, keepdims=True)
    var = jnp.mean(jnp.square(xf - mu), axis=-1, keepdims=True)
    y = (xf - mu) * lax.rsqrt(var + EPS)
    return (y * w.astype(jnp.float32) + b.astype(jnp.float32)).astype(x.dtype)


def _pre(x, norm_w, mod3):
    return _rms(x, norm_w) * (1 + mod3[:, 1][:, None]) + mod3[:, 0][:, None]


def _post_add(x, out, norm_w, gate, res_w):
    return x + res_w * gate[:, None] * _rms(out, norm_w)


def _swiglu(h, wg, wu, wd):
    return (jax.nn.silu(h @ wg) * (h @ wu)) @ wd


def _dwconv(x, w, bias):
    k = w.shape[0]
    out = lax.conv_general_dilated(
        x, w[:, None, :].astype(x.dtype), window_strides=(1,), padding=[(k // 2, k // 2)],
        dimension_numbers=("NWC", "WIO", "NWC"), feature_group_count=x.shape[-1])
    return out + bias.astype(x.dtype)


def _window_mean(v, axis, w):
    n = v.shape[axis]
    cs = jnp.cumsum(v, axis=axis)
    pad = [(0, 0)] * v.ndim
    pad[axis] = (1, 0)
    cs = jnp.pad(cs, pad)
    pos = jnp.arange(n)
    lo = jnp.clip(pos - w // 2, 0, n)
    hi = jnp.clip(pos + (w - w // 2), 0, n)
    s = jnp.take(cs, hi, axis=axis) - jnp.take(cs, lo, axis=axis)
    shape = [1] * v.ndim
    shape[axis] = n
    return s / (hi - lo).astype(v.dtype).reshape(shape)


def _pool_mixer(h, rows, pool_w, pool_scale):
    bsz, n, _ = h.shape
    hf = h.astype(jnp.float32).reshape(bsz, n, N_POOL_GROUPS, POOL_GROUP_DIM)
    outs = []
    for g, w in enumerate(POOL_WINDOWS):
        v = hf[:, :, g]
        if rows is None:
            m = _window_mean(v, 1, w)
        else:
            v2 = v.reshape(bsz, rows, GRID_W, POOL_GROUP_DIM)
            m = _window_mean(_window_mean(v2, 2, w), 1, w).reshape(bsz, n, POOL_GROUP_DIM)
        outs.append(m - v)
    pooled = jnp.stack(outs, axis=2).astype(h.dtype)
    mixed = jnp.einsum("blgc,gcd->blgd", pooled, pool_w).reshape(bsz, n, D_MODEL)
    return mixed * pool_scale


def _ssd_scan(xs, dt, a, bm, cm, h0):
    bsz, n, nh, hp = xs.shape
    nc = n // SSD_CHUNK
    r = nh // SSD_GROUPS
    x_dt = (xs * dt[..., None]).reshape(bsz, nc, SSD_CHUNK, SSD_GROUPS, r, hp)
    a_cum = jnp.cumsum((dt * a).reshape(bsz, nc, SSD_CHUNK, SSD_GROUPS, r), axis=2)
    b_c = bm.reshape(bsz, nc, SSD_CHUNK, SSD_GROUPS, SSD_STATE)
    c_c = cm.reshape(bsz, nc, SSD_CHUNK, SSD_GROUPS, SSD_STATE)
    lower = jnp.tril(jnp.ones((SSD_CHUNK, SSD_CHUNK), dtype=bool))[:, :, None, None]
    seg = a_cum[:, :, :, None] - a_cum[:, :, None, :]
    lmat = jnp.exp(jnp.where(lower, seg, -jnp.inf))
    cb = jnp.einsum("bclgn,bcsgn->bclsg", c_c, b_c)
    y_diag = jnp.einsum("bclsgr,bcsgrp->bclgrp", cb[..., None] * lmat, x_dt)
    decay_to_end = jnp.exp(a_cum[:, :, -1:] - a_cum)
    chunk_states = jnp.einsum("bclgn,bclgrp->bcgrpn", b_c, x_dt * decay_to_end[..., None])
    chunk_decay = jnp.exp(a_cum[:, :, -1])

    def step(carry, inp):
        st, dec = inp
        return carry * dec[..., None, None] + st, carry

    h_final, h_start = lax.scan(
        step, h0.reshape(bsz, SSD_GROUPS, r, hp, SSD_STATE),
        (jnp.moveaxis(chunk_states, 1, 0), jnp.moveaxis(chunk_decay, 1, 0)))
    h_start = jnp.moveaxis(h_start, 0, 1)
    y_off = jnp.einsum("bclgn,bcgrpn->bclgrp", c_c, h_start) * jnp.exp(a_cum)[..., None]
    y = (y_diag + y_off).reshape(bsz, n, nh, hp)
    return y, h_final.reshape(bsz, nh, hp, SSD_STATE)


def _flip(t):
    return jnp.flip(t, axis=1)


def _ssd_mixer(h, h0, in_w, conv_w, conv_b, a_log, dt_bias, d_skip, gnorm_w, out_w):
    bsz, n, _ = h.shape
    proj = h @ in_w
    z = proj[..., :SSD_D_INNER]
    xbc = proj[..., SSD_D_INNER:SSD_D_INNER + SSD_CONV_DIM]
    dt_raw = proj[..., SSD_D_INNER + SSD_CONV_DIM:]
    xbc = jax.nn.silu(_dwconv(xbc, conv_w, conv_b)).astype(jnp.float32)
    xs = xbc[..., :SSD_D_INNER].reshape(bsz, n, SSD_HEADS, SSD_HEAD_DIM)
    bm = xbc[..., SSD_D_INNER:SSD_D_INNER + SSD_BC_DIM].reshape(bsz, n, SSD_GROUPS, SSD_STATE)
    cm = xbc[..., SSD_D_INNER + SSD_BC_DIM:].reshape(bsz, n, SSD_GROUPS, SSD_STATE)
    dt = jax.nn.softplus(dt_raw.astype(jnp.float32).reshape(bsz, n, 2, SSD_HEADS)
                         + dt_bias.astype(jnp.float32))
    a = -jnp.exp(a_log.astype(jnp.float32))
    y_f, s_f = _ssd_scan(xs, dt[:, :, 0], a[0], bm, cm, h0[:, 0])
    y_b, s_b = _ssd_scan(_flip(xs), _flip(dt[:, :, 1]), a[1], _flip(bm), _flip(cm), h0[:, 1])
    y = y_f + _flip(y_b) + d_skip.astype(jnp.float32)[:, None] * xs
    y = y.reshape(bsz, n, SSD_D_INNER) * jax.nn.silu(z.astype(jnp.float32))
    y = _rms(y, gnorm_w).astype(h.dtype)
    return y @ out_w, jnp.stack([s_f, s_b], axis=1)


def _conv_module(h, pw1, dw_w, dw_b, ln_w, ln_b, pw2):
    u = h @ pw1
    u = u[..., :D_MODEL] * jax.nn.sigmoid(u[..., D_MODEL:])
    u = _dwconv(u, dw_w, dw_b)
    u = jax.nn.silu(_layernorm(u, ln_w, ln_b))
    return u @ pw2


def _trunk(x, cond, rows, h0_all, norm_w, ada_w, ada_b, ffn_wg, ffn_wu, ffn_wd,
           pool_w, pool_scale, ssd_in_w, ssd_conv_w, ssd_conv_b, ssd_a_log, ssd_dt_bias,
           ssd_d, ssd_norm_w, ssd_out_w, cm_pw1, cm_dw_w, cm_dw_b, cm_ln_w, cm_ln_b, cm_pw2):
    bsz = x.shape[0]
    sc = jax.nn.silu(cond)
    states = []
    for i in range(DEPTH):
        mod = (sc @ ada_w[i] + ada_b[i]).reshape(cond.shape[0], N_MOD, D_MODEL)
        h = _pre(x, norm_w[i, 0, 0], mod[:, 0:3])
        x = _post_add(x, _swiglu(h, ffn_wg[i, 0], ffn_wu[i, 0], ffn_wd[i, 0]),
                      norm_w[i, 0, 1], mod[:, 2], 0.5)
        kind, j = i % N_MIXERS, i // N_MIXERS
        h = _pre(x, norm_w[i, 1, 0], mod[:, 3:6])
        if kind == 0:
            out = _pool_mixer(h, rows, pool_w[j], pool_scale[j])
        elif kind == 1:
            if h0_all is None:
                h0 = jnp.zeros((bsz, 2, SSD_HEADS, SSD_HEAD_DIM, SSD_STATE), jnp.float32)
            else:
                h0 = h0_all[:, j].astype(jnp.float32)
            out, st = _ssd_mixer(h, h0, ssd_in_w[j], ssd_conv_w[j], ssd_conv_b[j], ssd_a_log[j],
                                 ssd_dt_bias[j], ssd_d[j], ssd_norm_w[j], ssd_out_w[j])
            states.append(st)
        else:
            out = _conv_module(h, cm_pw1[j], cm_dw_w[j], cm_dw_b[j], cm_ln_w[j], cm_ln_b[j], cm_pw2[j])
        x = _post_add(x, out, norm_w[i, 1, 1], mod[:, 5], 1.0)
        h = _pre(x, norm_w[i, 2, 0], mod[:, 6:9])
        x = _post_add(x, _swiglu(h, ffn_wg[i, 1], ffn_wu[i, 1], ffn_wd[i, 1]),
                      norm_w[i, 2, 1], mod[:, 8], 0.5)
    return x, states


def setup_inputs(seed: int = 0) -> dict:
    key = jax.random.key(seed)
    ks = jax.random.split(key, 32)
    f32 = jnp.float32

    def nrm(k, shape, scale):
        return jax.random.normal(k, shape, f32) * scale

    dt0 = jnp.exp(jax.random.uniform(ks[17], (N_SSD_LAYERS, 2, SSD_HEADS), f32,
                                     np.log(1e-3).astype(np.float32), np.log(1e-1).astype(np.float32)))
    return {
        "x_prompt": nrm(ks[0], (BATCH, SEQ, D_MODEL), 1.0),
        "x_sample": nrm(ks[1], (DEC_BATCH, DEC_SEQ, D_MODEL), 1.0),
        "state_ssd": nrm(ks[2], (DEC_BATCH, N_SSD_LAYERS, 2, SSD_HEADS, SSD_HEAD_DIM, SSD_STATE), 0.1),
        "c": nrm(ks[3], (DEC_BATCH, D_MODEL), 1.0),
        "c_ctx": nrm(ks[4], (D_MODEL,), 1.0),
        "norm_w": 1.0 + nrm(ks[5], (DEPTH, 3, 2, D_MODEL), 0.02),
        "ada_w": nrm(ks[6], (DEPTH, D_MODEL, N_MOD * D_MODEL), 0.5 * D_MODEL ** -0.5),
        "ada_b": nrm(ks[7], (DEPTH, N_MOD * D_MODEL), 0.02),
        "ffn_wg": nrm(ks[8], (DEPTH, 2, D_MODEL, D_FF), D_MODEL ** -0.5),
        "ffn_wu": nrm(ks[9], (DEPTH, 2, D_MODEL, D_FF), D_MODEL ** -0.5),
        "ffn_wd": nrm(ks[10], (DEPTH, 2, D_FF, D_MODEL), D_FF ** -0.5),
        "pool_w": nrm(ks[11], (N_POOL_LAYERS, N_POOL_GROUPS, POOL_GROUP_DIM, POOL_GROUP_DIM), POOL_GROUP_DIM ** -0.5),
        "pool_scale": 1.0 + nrm(ks[12], (N_POOL_LAYERS, D_MODEL), 0.02),
        "ssd_in_w": nrm(ks[13], (N_SSD_LAYERS, D_MODEL, SSD_IN_DIM), D_MODEL ** -0.5),
        "ssd_conv_w": nrm(ks[14], (N_SSD_LAYERS, SSD_CONV, SSD_CONV_DIM), SSD_CONV ** -0.5),
        "ssd_conv_b": nrm(ks[15], (N_SSD_LAYERS, SSD_CONV_DIM), 0.02),
        "ssd_a_log": jnp.log(jax.random.uniform(ks[16], (N_SSD_LAYERS, 2, SSD_HEADS), f32, 1.0, 16.0)),
        "ssd_dt_bias": dt0 + jnp.log(-jnp.expm1(-dt0)),
        "ssd_d": 1.0 + nrm(ks[18], (N_SSD_LAYERS, SSD_HEADS), 0.02),
        "ssd_norm_w": 1.0 + nrm(ks[19], (N_SSD_LAYERS, SSD_D_INNER), 0.02),
        "ssd_out_w": nrm(ks[20], (N_SSD_LAYERS, SSD_D_INNER, D_MODEL), SSD_D_INNER ** -0.5),
        "cm_pw1": nrm(ks[21], (N_CONV_LAYERS, D_MODEL, 2 * D_MODEL), D_MODEL ** -0.5),
        "cm_dw_w": nrm(ks[22], (N_CONV_LAYERS, CM_KERNEL, D_MODEL), CM_KERNEL ** -0.5),
        "cm_dw_b": nrm(ks[23], (N_CONV_LAYERS, D_MODEL), 0.02),
        "cm_ln_w": 1.0 + nrm(ks[24], (N_CONV_LAYERS, D_MODEL), 0.02),
        "cm_ln_b": nrm(ks[25], (N_CONV_LAYERS, D_MODEL), 0.02),
        "cm_pw2": nrm(ks[26], (N_CONV_LAYERS, D_MODEL, D_MODEL), D_MODEL ** -0.5),
    }


def reference(x_prompt, x_sample, state_ssd, c, c_ctx, norm_w, ada_w, ada_b, ffn_wg, ffn_wu,
              ffn_wd, pool_w, pool_scale, ssd_in_w, ssd_conv_w, ssd_conv_b, ssd_a_log,
              ssd_dt_bias, ssd_d, ssd_norm_w, ssd_out_w, cm_pw1, cm_dw_w, cm_dw_b, cm_ln_w,
              cm_ln_b, cm_pw2):
    params = (norm_w, ada_w, ada_b, ffn_wg, ffn_wu, ffn_wd, pool_w, pool_scale, ssd_in_w,
              ssd_conv_w, ssd_conv_b, ssd_a_log, ssd_dt_bias, ssd_d, ssd_norm_w, ssd_out_w,
              cm_pw1, cm_dw_w, cm_dw_b, cm_ln_w, cm_ln_b, cm_pw2)
    y_prompt, ctx_states = _trunk(x_prompt, c_ctx[None, :], None, None, *params)
    new_state_ssd = jnp.stack(ctx_states, axis=1).astype(x_prompt.dtype)
    rows = x_sample.shape[1] // GRID_W
    y_sample, _ = _trunk(x_sample, c, rows, state_ssd, *params)
    return (y_prompt, y_sample, new_state_ssd)
```

```python
import numpy as np
import ml_dtypes
from contextlib import ExitStack
import concourse.bass as bass
import concourse.mybir as mybir
from concourse.bass_utils import run_bass_kernel_spmd

F32 = mybir.dt.float32
BF16 = mybir.dt.bfloat16
AF = mybir.ActivationFunctionType
ALU = mybir.AluOpType

D = 2048
KC = 16
T = 2048
TT = 512
NT = T // TT
FF = 5632
FC = FF // 128
DEPTH = 4
EPS = 1e-6
NCORES = 8

SB_BASE = 16512
SB_END = 229376
PERSIST = 10 * 1024


class Slot:
    def __init__(self, P, name, shape, dt):
        self.name = name
        self.t = P.sb(name, shape, dt)
        self.sem = P.sem(name + "_s")
        self.P = P
        self.last_use = P.last

    @property
    def cnt(self):
        return self.P.sem_cnt[self.name + "_s"]

    @cnt.setter
    def cnt(self, v):
        self.P.sem_cnt[self.name + "_s"] = v

    def release(self, P):
        self.last_use = P.last


class Prog:
    def __init__(self, nc, es):
        self.nc = nc
        self.es = es
        self.ops = {"pe": [], "act": [], "dve": [], "sp": [], "pool": []}
        self.sem_cache = {}
        self.sem_cnt = {}
        self.esem = {e: self.sem("S_" + e) for e in ("pe", "act", "dve", "sp")}
        self.ecnt = {"pe": 0, "act": 0, "dve": 0, "sp": 0}
        self.last = None
        self.n = 0
        self.buf_store = {}
        self.dram_store = {}
        self.store_sems = {}
        self.store_cnt = {}
        self.uid = 0
        self.persist_off = SB_BASE
        self.arena_off = SB_BASE + PERSIST
        self.psall_t = es.enter_context(nc.psum_tensor("psall", [128, 4096], F32))
        self.psall = self.psall_t[:, :]
        self.psum = [self.psall_t[:, i * 512:(i + 1) * 512] for i in range(8)]

    def sem(self, name):
        if name not in self.sem_cache:
            self.sem_cache[name] = self.es.enter_context(self.nc.semaphore(name))
            self.sem_cnt[name] = 0
        return self.sem_cache[name]

    def _alloc(self, name, shape, dt, off):
        self.uid += 1
        return self.nc.alloc_sbuf_tensor_at("%s_%d" % (name, self.uid), list(shape), dt, offset=off)

    @staticmethod
    def _bytes(shape, dt):
        n = 1
        for s in shape[1:]:
            n *= s
        b = n * (2 if dt == BF16 else 4)
        return (b + 63) // 64 * 64

    def persist(self, name, shape, dt):
        off = self.persist_off
        self.persist_off += self._bytes(shape, dt)
        assert self.persist_off <= SB_BASE + PERSIST, "persist overflow"
        return self._alloc(name, shape, dt, off)

    def sb(self, name, shape, dt):
        off = self.arena_off
        self.arena_off += self._bytes(shape, dt)
        assert self.arena_off <= SB_END, "arena overflow %s %d" % (name, self.arena_off)
        return self._alloc(name, shape, dt, off)

    def chain(self, eng, fn, waits=(), wbufs=(), deps=None):
        idx = self.ecnt[eng]
        self.ecnt[eng] += 1
        d = [self.last] if deps is None else list(deps)
        d = [x for x in d if x is not None]
        ws = list(waits)
        for b in wbufs:
            if b in self.buf_store:
                ws.append(self.buf_store.pop(b))
        esem = self.esem

        def emit(e):
            for (de, di) in d:
                e.wait_ge(esem[de], di + 1)
            for (s, v) in ws:
                e.wait_ge(s, v)
            fn(e).then_inc(esem[eng], 1)

        self.ops[eng].append(emit)
        self.n += 1
        self.last = (eng, idx)
        return self.last

    def join_deps(self):
        return [(e, c - 1) for e, c in self.ecnt.items() if c > 0]

    def load(self, q, slot, pairs, dram_names=()):
        free_after = slot.last_use
        ws = []
        for dn in dram_names:
            for sv in self.dram_store.get(dn, {}).values():
                ws.append(sv)
        if slot.name in self.buf_store:
            ws.append(self.buf_store.pop(slot.name))
        esem = self.esem
        sem = slot.sem

        def emit(e):
            if free_after is not None:
                e.wait_ge(esem[free_after[0]], free_after[1] + 1)
            for (s, v) in ws:
                e.wait_ge(s, v)
            for dst, src in pairs:
                e.dma_start(out=dst, in_=src).then_inc(sem, 16)

        slot.cnt += 16 * len(pairs)
        self.ops[q].append(emit)
        return (slot.sem, slot.cnt)

    def store(self, q, bufname, pairs, dram_name):
        if bufname not in self.store_sems:
            self.store_sems[bufname] = self.sem("st_" + bufname)
            self.store_cnt[bufname] = 0
        sem = self.store_sems[bufname]
        jd = self.join_deps()
        esem = self.esem

        def emit(e):
            for (de, di) in jd:
                e.wait_ge(esem[de], di + 1)
            for dst, src in pairs:
                e.dma_start(out=dst, in_=src).then_inc(sem, 16)

        self.store_cnt[bufname] += 16 * len(pairs)
        tok = (sem, self.store_cnt[bufname])
        self.buf_store[bufname] = tok
        self.dram_store.setdefault(dram_name, {})[bufname] = tok
        self.ops[q].append(emit)

    def store_after(self, q, bufname, pairs, dram_name, deps):
        if bufname not in self.store_sems:
            self.store_sems[bufname] = self.sem("st_" + bufname)
            self.store_cnt[bufname] = 0
        sem = self.store_sems[bufname]
        jd = [x for x in deps if x is not None]
        esem = self.esem

        def emit(e):
            for (de, di) in jd:
                e.wait_ge(esem[de], di + 1)
            for dst, src in pairs:
                e.dma_start(out=dst, in_=src).then_inc(sem, 16)

        self.store_cnt[bufname] += 16 * len(pairs)
        tok = (sem, self.store_cnt[bufname])
        self.buf_store[bufname] = tok
        self.dram_store.setdefault(dram_name, {})[bufname] = tok
        self.ops[q].append(emit)

    def barrier(self):
        ws = [(s, self.store_cnt[b]) for b, s in self.store_sems.items()]
        self.chain("sp", lambda e: e.nop(), waits=ws, deps=self.join_deps())
        self.buf_store = {}
        self.dram_store = {}
        self.arena_off = SB_BASE + PERSIST

    def final_wait(self):
        ws = [(s, self.store_cnt[b]) for b, s in self.store_sems.items()]
        jd = self.join_deps()
        esem = self.esem

        def emit(e):
            for (de, di) in jd:
                e.wait_ge(esem[de], di + 1)
            for (s, v) in ws:
                e.wait_ge(s, v)

        self.ops["sp"].append(emit)


def bc_mid(ap2d, k):
    f = ap2d.shape[-1]
    return ap2d.unsqueeze(1).to_broadcast([128, k, f])


class Pipe:
    def __init__(self, P):
        self.P = P
        self.i = 0
        self.ev = {}

    def parity(self):
        return self.i % 2

    def pe(self, fn, waits=()):
        P = self.P
        deps = [P.last] if self.i == 0 else []
        if self.i >= 2:
            deps.append(self.ev[self.i - 2])
        self.hpe = P.chain("pe", fn, waits=waits, deps=deps)
        self.cur = self.hpe
        return self.hpe

    def evac(self, eng, fn, waits=(), wbufs=()):
        self.cur = self.P.chain(eng, fn, waits=waits, wbufs=wbufs, deps=[self.cur])
        return self.cur

    def next(self):
        self.ev[self.i] = self.cur
        self.i += 1

    def end(self):
        P = self.P
        P.chain("sp", lambda e: e.nop(), deps=P.join_deps())


def build_program(stages, debug_out=False):
    nc = bass.Bass("TRN2", target_bir_lowering=False)
    es = ExitStack()
    P = Prog(nc, es)

    def din(name, shape, dt=F32):
        return nc.dram_tensor(name, list(shape), dt, kind="ExternalInput").ap()

    xT_in = din("xT", [D, T])
    cvec = din("cvec", [128, KC])
    ada_w = din("ada_w", [DEPTH, D, 9 * D])
    ada_bT = din("ada_bT", [128, DEPTH * 9 * KC])
    nwT_in = din("nwT", [128, DEPTH * 3 * 2 * KC])
    wg = din("wg", [8, FC, 128, KC * 128])
    wu = din("wu", [8, FC, 128, KC * 128])
    wd = din("wd", [8, KC, 128, FC * 128])
    pool_wT = din("pool_wT", [2, 128, 4 * 4 * 512])
    pool_scT = din("pool_scT", [128, 2 * KC])
    poolP = din("poolP", [4, T, T], BF16)
    pw1 = din("pw1", [KC, 2, 128, KC * 128])
    cm_diag = din("cm_diag", [KC, 128, 31 * 128])
    cm_vecT = din("cm_vecT", [128, 3 * KC])
    pw2 = din("pw2", [KC, 128, KC * 128])
    cmask = din("cmask", [128, 1])
    ssd_in = din("ssd_in", [80, 128, KC * 128])
    ssd_wdt = din("ssd_wdt", [128, KC * 128])
    ssd_diag = din("ssd_diag", [48, 128, 5 * 128])
    ssd_cbT = din("ssd_cbT", [128, 48])
    dtb_bc = din("dtb_bc", [128, 128])
    alog_bc = din("alog_bc", [128, 128])
    ssd_dgT = din("ssd_dgT", [128, 64])
    ssd_out = din("ssd_out", [KC, 128, 32 * 128])
    h0T = din("h0T", [2, 128, 4096])
    scanmask = din("scanmask", [128, 32])
    ident = din("ident", [128, 128])
    tri_in = din("tri_in", [2, 128, 128])
    st_out = nc.dram_tensor("st_out", [8, 2, 128, 4096], F32, kind="ExternalOutput").ap()

    def dscr(name, shape, dt):
        return nc.dram_tensor(name, list(shape), dt, kind="Internal").ap()
    gls = dscr("gls", [D, T + 30], BF16)
    zs_d = dscr("zs_d", [4096, T], BF16)
    xbc_raw_d = dscr("xbc_raw_d", [6144, T + 4], BF16)
    dt_tok_d = dscr("dt_tok_d", [T, 128], F32)
    xtok_d = dscr("xtok_d", [T, 5120], BF16)
    xcT_d = dscr("xcT_d", [6144, T], BF16)
    yf_d = dscr("yf_d", [4096, T], F32)
    yb_d = dscr("yb_d", [4096, T], F32)
    yT = nc.dram_tensor("yT", [D, T], F32, kind="ExternalOutput").ap()
    xs = nc.dram_tensor("xs", [D, T], F32, kind="Internal").ap()

    def tile_view(ap, tt):
        return ap.rearrange("(k p) t -> p k t", p=128)[:, :, tt * TT:(tt + 1) * TT]

    ones_bf = P.persist("ones_bf", [128, 128], BF16)
    nwT = P.persist("nwT", [128, DEPTH * 3 * 2 * KC], F32)
    modT = P.persist("modT", [128, DEPTH * 9 * KC], F32)
    coefA = P.persist("coefA", [128, DEPTH * 3 * KC], F32)
    coefC = P.persist("coefC", [128, DEPTH * 3 * KC], F32)
    scb = P.persist("scb", [128, KC], BF16)
    ps = P.psum

    def nw_ap(li, j, w):
        o = ((li * 3 + j) * 2 + w) * KC
        return nwT[:, o:o + KC]

    def mod_ap(li, m):
        o = (li * 9 + m) * KC
        return modT[:, o:o + KC]

    def cf(t, li, j):
        o = (li * 3 + j) * KC
        return t[:, o:o + KC]

    def stage_adaln():
        cslot = Slot(P, "cslot", [128, KC], F32)
        bslot = Slot(P, "bslot", [128, DEPTH * 9 * KC], F32)
        nslot = Slot(P, "nslot", [128, DEPTH * 3 * 2 * KC], F32)
        tc_ = P.load("sp", cslot, [(cslot.t[:, :], cvec)])
        tb = P.load("sp", bslot, [(bslot.t[:, :], ada_bT)])
        tn = P.load("sp", nslot, [(nslot.t[:, :], nwT_in)])
        P.chain("dve", lambda e: e.memset(ones_bf[:, :], 1.0))
        P.chain("act", lambda e: e.activation(out=scb[:, :], in_=cslot.t[:, :], func=AF.Silu), waits=[tc_])
        P.chain("dve", lambda e: e.tensor_copy(nwT[:, :], nslot.t[:, :]), waits=[tn])
        NQ = 6
        QW = 9 * D // NQ
        NB = QW // 512
        wsl = [Slot(P, "adaw%d" % i, [128, QW], BF16) for i in range(4)]
        row = P.sb("modrow", [1, QW], F32)
        rhi = P.sb("rowhi", [1, QW], BF16)
        rlo = P.sb("rowlo", [1, QW], BF16)
        rtmp = P.sb("rowtmp", [1, QW], F32)
        cnt = 0
        for li in range(DEPTH):
            for q in range(NQ):
                for k in range(KC):
                    sl = wsl[cnt % 4]
                    cnt += 1
                    tok = P.load("pool", sl, [(sl.t[:, :], ada_w[li, k * 128:(k + 1) * 128, q * QW:(q + 1) * QW])])

                    def mm(e, sl=sl, k=k):
                        r = None
                        for b in range(NB):
                            r = e.matmul(ps[b][0:1, :], scb[:, k:k + 1], sl.t[:, b * 512:(b + 1) * 512],
                                         start=(k == 0), stop=(k == KC - 1))
                        return r
                    P.chain("pe", mm, waits=[tok])
                    sl.release(P)

                def ev(e):
                    r = None
                    for b in range(NB):
                        r = e.activation(out=row[0:1, b * 512:(b + 1) * 512], in_=ps[b][0:1, :], func=AF.Copy)
                    return r
                P.chain("act", ev)
                P.chain("dve", lambda e: e.tensor_copy(rhi[:, :], row[:, :]))
                P.chain("dve", lambda e: e.tensor_tensor(rtmp[:, :], row[:, :], rhi[:, :], op=ALU.subtract))
                P.chain("dve", lambda e: e.tensor_copy(rlo[:, :], rtmp[:, :]))
                ncol = QW // 128

                def tr(e):
                    r = None
                    for c in range(ncol):
                        e.matmul(ps[6][:, c:c + 1], rhi[0:1, c * 128:(c + 1) * 128], ones_bf[0:1, 0:1],
                                 start=True, stop=False)
                        r = e.matmul(ps[6][:, c:c + 1], rlo[0:1, c * 128:(c + 1) * 128], ones_bf[0:1, 0:1],
                                     start=False, stop=True)
                    return r
                P.chain("pe", tr)
                o = li * 9 * KC + q * ncol
                P.chain("act", lambda e, o=o: e.activation(out=modT[:, o:o + ncol], in_=ps[6][:, 0:ncol], func=AF.Copy))
        P.chain("dve", lambda e: e.tensor_tensor(modT[:, :], modT[:, :], bslot.t[:, :], op=ALU.add), waits=[tb])

        def coefs(e):
            r = None
            for li in range(DEPTH):
                for j in range(3):
                    rw = 1.0 if j == 1 else 0.5
                    e.scalar_tensor_tensor(out=cf(coefA, li, j), in0=mod_ap(li, 3 * j + 1), scalar=1.0,
                                           in1=nw_ap(li, j, 0), op0=ALU.add, op1=ALU.mult)
                    r = e.scalar_tensor_tensor(out=cf(coefC, li, j), in0=mod_ap(li, 3 * j + 2), scalar=rw,
                                               in1=nw_ap(li, j, 1), op0=ALU.mult, op1=ALU.mult)
            return r
        P.chain("dve", coefs)
        P.barrier()

    def rms_stats(src3, sq3, rstd2, lnv2, dim, waits=()):
        nch = src3.shape[1]
        W = src3.shape[2]
        P.chain("act", lambda e: e.activation(out=sq3, in_=src3, func=AF.Square), waits=waits)

        def mm(e):
            r = None
            for k in range(nch):
                r = e.matmul(ps[7][:, 0:W], ones_bf[:, :], sq3[:, k, :], start=(k == 0), stop=(k == nch - 1))
            return r
        P.chain("pe", mm)
        P.chain("act", lambda e: e.activation(out=lnv2, in_=ps[7][:, 0:W], func=AF.Ln, bias=EPS, scale=1.0 / dim))
        P.chain("act", lambda e: e.activation(out=rstd2, in_=lnv2, func=AF.Exp, scale=-0.5))

    def prenorm(xt3, tmp3, h3, sq3, rstd2, lnv2, li, j, waits=()):
        rms_stats(xt3, sq3, rstd2, lnv2, D, waits)
        P.chain("dve", lambda e: e.tensor_tensor(tmp3, xt3, bc_mid(rstd2, KC), op=ALU.mult))
        A = cf(coefA, li, j)
        B = mod_ap(li, 3 * j)

        def aff(e):
            r = None
            for k in range(KC):
                r = e.tensor_scalar(h3[:, k, :], tmp3[:, k, :], A[:, k:k + 1], B[:, k:k + 1], op0=ALU.mult, op1=ALU.add)
            return r
        P.chain("dve", aff)

    def postnorm_add(xt3, o3, sq3, rstd2, lnv2, li, j, waits=()):
        rms_stats(o3, sq3, rstd2, lnv2, D)
        P.chain("dve", lambda e: e.tensor_tensor(o3, o3, bc_mid(rstd2, KC), op=ALU.mult))
        C = cf(coefC, li, j)

        def upd(e):
            r = None
            for k in range(KC):
                r = e.scalar_tensor_tensor(out=xt3[:, k, :], in0=o3[:, k, :], scalar=C[:, k:k + 1], in1=xt3[:, k, :],
                                           op0=ALU.mult, op1=ALU.add)
            return r
        P.chain("dve", upd, waits=waits)

    def stage_ffn(li, j, src, dst, src_name, dst_name):
        fi = 2 * li + (0 if j == 0 else 1)
        xsl = Slot(P, "xt", [128, KC, TT], F32)
        o = P.sb("o", [128, KC, TT], F32)
        h = P.sb("h", [128, KC, TT], BF16)
        act = P.sb("act", [128, FC, TT], BF16)
        sq = act[:, 0:KC, :]
        rstd = P.sb("rstd", [128, TT], F32)
        lnv = P.sb("lnv", [128, TT], F32)
        sg2 = [P.sb("sg%d" % i, [128, TT], F32) for i in range(2)]
        NWGU = 3
        wgu = [Slot(P, "wgu%d" % i, [128, 2, KC, 128], BF16) for i in range(NWGU)]
        wds = [Slot(P, "wd%d" % i, [128, FC, 128], BF16) for i in range(2)]
        xt3 = xsl.t[:, :, :]
        o3 = o[:, :, :]
        cw = 0
        cd = 0
        for tt in range(NT):
            tx = P.load("sp", xsl, [(xt3, tile_view(src, tt))], dram_names=[src_name])
            prenorm(xt3, o3, h[:, :, :], sq, rstd[:, :], lnv[:, :], li, j, waits=[tx])
            dveh = {}
            for f in range(FC):
                sl = wgu[cw % NWGU]
                cw += 1
                tok = P.load("pool", sl, [
                    (sl.t[:, 0, :, :], wg[fi, f].rearrange("p (k j) -> p k j", j=128)),
                    (sl.t[:, 1, :, :], wu[fi, f].rearrange("p (k j) -> p k j", j=128))])
                pp = f % 2
                bg, bu = ps[2 * pp], ps[2 * pp + 1]

                def mm(e, sl=sl, bg=bg, bu=bu):
                    r = None
                    for gu, bank in ((0, bg), (1, bu)):
                        for k in range(KC):
                            r = e.matmul(bank[:, :], sl.t[:, gu, k, :], h[:, k, :], start=(k == 0), stop=(k == KC - 1))
                    return r
                deps = [P.last] if f == 0 else []
                if f >= 2:
                    deps.append(dveh[f - 2])
                hpe = P.chain("pe", mm, waits=[tok], deps=deps)
                sl.release(P)
                sgp = sg2[pp]
                hact = P.chain("act", lambda e, sgp=sgp, bg=bg: e.activation(out=sgp[:, :], in_=bg[:, :], func=AF.Silu), deps=[hpe])
                dveh[f] = P.chain("dve", lambda e, f=f, sgp=sgp, bu=bu: e.tensor_tensor(act[:, f, :], sgp[:, :], bu[:, :], op=ALU.mult), deps=[hact])
            acth = {}
            for dc in range(KC):
                sl = wds[cd % 2]
                cd += 1
                tok = P.load("pool", sl, [(sl.t[:, :, :], wd[fi, dc].rearrange("p (k j) -> p k j", j=128))])
                bo = ps[4 + dc % 2]

                def mm2(e, sl=sl, bo=bo):
                    r = None
                    for fk in range(FC):
                        r = e.matmul(bo[:, :], sl.t[:, fk, :], act[:, fk, :], start=(fk == 0), stop=(fk == FC - 1))
                    return r
                deps = [dveh[FC - 1]] if dc == 0 else []
                if dc >= 2:
                    deps.append(acth[dc - 2])
                hpe = P.chain("pe", mm2, waits=[tok], deps=deps)
                sl.release(P)
                acth[dc] = P.chain("act", lambda e, dc=dc, bo=bo: e.activation(out=o[:, dc, :], in_=bo[:, :], func=AF.Copy), deps=[hpe])
            postnorm_add(xt3, o3, sq, rstd[:, :], lnv[:, :], li, j)
            P.store("sp", "xt", [(tile_view(dst, tt), xt3)], dst_name)
            xsl.release(P)
        P.barrier()

    def stage_ffn2(li, j, src, dst, src_name, dst_name):
        fi = 2 * li + (0 if j == 0 else 1)
        TP = 2 * TT
        FH = FC // 2
        h = P.sb("h", [128, KC, TP], BF16)
        off_act = P.arena_off
        actH = P.sb("actH", [128, FH, TP], BF16)
        off_o = P.arena_off
        o = P.sb("o", [128, KC, TP], F32)
        xA_t = P._alloc("xA", [128, KC, TT], F32, off_o)
        sqA = P._alloc("sqA", [128, KC, TT], BF16, off_o + 32768)
        tmpA = P._alloc("tmpA", [128, KC, TT], F32, off_act)
        xB_t = P._alloc("xB", [128, KC, TT], F32, off_act)
        sqB = h[:, :, 0:TT]
        xA = Slot.__new__(Slot); xA.name = "xA"; xA.t = xA_t; xA.sem = P.sem("xA_s"); xA.P = P; xA.last_use = P.last
        xB = Slot.__new__(Slot); xB.name = "xB"; xB.t = xB_t; xB.sem = P.sem("xB_s"); xB.P = P; xB.last_use = P.last
        rstd = P.sb("rstd", [128, TT], F32)
        lnv = P.sb("lnv", [128, TT], F32)
        sg2 = [P.sb("sg%d" % i, [128, TT], F32) for i in range(2)]
        NWGU = 3
        wgu = [Slot(P, "wgu%d" % i, [128, 2, KC, 128], BF16) for i in range(NWGU)]
        wds = [Slot(P, "wdh%d" % i, [128, FH, 128], BF16) for i in range(2)]
        cw = 0
        cd = 0
        for pr in range(T // TP):
            if pr > 0:
                P.chain("sp", lambda e: e.nop(), wbufs=["xB"])
            for hf in range(2):
                tt = pr * 2 + hf
                tx = P.load("sp", xA, [(xA_t[:, :, :], tile_view(src, tt))], dram_names=[src_name])
                prenorm(xA_t[:, :, :], tmpA[:, :, :], h[:, :, hf * TT:(hf + 1) * TT], sqA[:, :, :], rstd[:, :], lnv[:, :], li, j, waits=[tx])
                xA.release(P)
            hprev = None
            for fh in range(2):
                dveh = {}
                u = 0
                for f in range(FH):
                    fg = fh * FH + f
                    sl = wgu[cw % NWGU]
                    cw += 1
                    tok = P.load("pool", sl, [
                        (sl.t[:, 0, :, :], wg[fi, fg].rearrange("p (k j) -> p k j", j=128)),
                        (sl.t[:, 1, :, :], wu[fi, fg].rearrange("p (k j) -> p k j", j=128))])
                    for hf in range(2):
                        pp = u % 2
                        bg, bu = ps[2 * pp], ps[2 * pp + 1]

                        def mm(e, sl=sl, bg=bg, bu=bu, hf=hf):
                            r = None
                            for gu, bank in ((0, bg), (1, bu)):
                                for k in range(KC):
                                    r = e.matmul(bank[:, :], sl.t[:, gu, k, :], h[:, k, hf * TT:(hf + 1) * TT], start=(k == 0), stop=(k == KC - 1))
                            return r
                        deps = [P.last] if u == 0 else []
                        if u >= 2:
                            deps.append(dveh[u - 2])
                        hpe = P.chain("pe", mm, waits=[tok] if hf == 0 else [], deps=deps)
                        sgp = sg2[pp]
                        hact = P.chain("act", lambda e, sgp=sgp, bg=bg: e.activation(out=sgp[:, :], in_=bg[:, :], func=AF.Silu), deps=[hpe])
                        dveh[u] = P.chain("dve", lambda e, f=f, hf=hf, sgp=sgp, bu=bu: e.tensor_tensor(actH[:, f, hf * TT:(hf + 1) * TT], sgp[:, :], bu[:, :], op=ALU.mult),
                                          deps=[hact])
                        u += 1
                    sl.last_use = hpe
                evh = {}
                u = 0
                for dc in range(KC):
                    sl = wds[cd % 2]
                    cd += 1
                    tok = P.load("pool", sl, [(sl.t[:, :, :], wd[fi, dc][:, fh * FH * 128:(fh + 1) * FH * 128].rearrange("p (k j) -> p k j", j=128))])
                    for hf in range(2):
                        bo = ps[4 + u % 2]

                        def mm2(e, sl=sl, bo=bo, hf=hf):
                            r = None
                            for fk in range(FH):
                                r = e.matmul(bo[:, :], sl.t[:, fk, :], actH[:, fk, hf * TT:(hf + 1) * TT], start=(fk == 0), stop=(fk == FH - 1))
                            return r
                        deps = [dveh[2 * FH - 1]] if u == 0 else []
                        if u >= 2:
                            deps.append(evh[u - 2])
                        hpe = P.chain("pe", mm2, waits=[tok] if hf == 0 else [], deps=deps)
                        osl = o[:, dc, hf * TT:(hf + 1) * TT]
                        if fh == 0:
                            evh[u] = P.chain("act", lambda e, osl=osl, bo=bo: e.activation(out=osl, in_=bo[:, :], func=AF.Copy), deps=[hpe])
                        else:
                            evh[u] = P.chain("dve", lambda e, osl=osl, bo=bo: e.tensor_tensor(osl, osl, bo[:, :], op=ALU.add), deps=[hpe])
                        u += 1
                    sl.last_use = hpe
                P.chain("sp", lambda e: e.nop(), deps=P.join_deps())
            xB.last_use = P.last
            for hf in range(2):
                tt = pr * 2 + hf
                tx = P.load("sp", xB, [(xB_t[:, :, :], tile_view(src, tt))], dram_names=[src_name])
                postnorm_add(xB_t[:, :, :], o[:, :, hf * TT:(hf + 1) * TT], sqB, rstd[:, :], lnv[:, :], li, j, waits=[tx])
                P.store("sp", "xB", [(tile_view(dst, tt), xB_t[:, :, :])], dst_name)
                xB.release(P)
            xA.last_use = P.last
        P.barrier()

    def stage_pool(li):
        pl_ = li // 3
        j = 1
        U = P.sb("U", [128, 16, 2048], BF16)
        Wp = Slot(P, "Wp", [128, 4, 4, 512], BF16)
        psc = Slot(P, "psc", [128, 2 * KC], F32)
        xsl = Slot(P, "xt", [128, KC, TT], F32)
        o = P.sb("o", [128, KC, TT], F32)
        h = P.sb("h", [128, KC, TT], BF16)
        sq = h[:, :, :]
        rstd = P.sb("rstd", [128, TT], F32)
        lnv = P.sb("lnv", [128, TT], F32)
        pm = [Slot(P, "pm%d" % i, [128, 16, TT], BF16) for i in range(2)]
        xt3 = xsl.t[:, :, :]
        o3 = o[:, :, :]
        tw = P.load("pool", Wp, [(Wp.t[:, :, :, :], pool_wT[pl_].rearrange("p (g k d) -> p g k d", g=4, k=4))])
        tsc = P.load("sp", psc, [(psc.t[:, :], pool_scT)])
        for tt in range(NT):
            tx = P.load("sp", xsl, [(xt3, tile_view(xs, tt))], dram_names=["xs"])
            prenorm(xt3, o3, h[:, :, :], sq, rstd[:, :], lnv[:, :], li, j, waits=[tx])
            pl = Pipe(P)
            for tc in range(4):
                bo = 4 * pl.parity()

                def mm(e, tc=tc, bo=bo):
                    r = None
                    for g in range(4):
                        for k in range(4):
                            r = e.matmul(ps[bo + g][:, :], h[:, g * 4 + k, tc * 128:(tc + 1) * 128], Wp.t[:, g, k, :],
                                         start=(k == 0), stop=(k == 3))
                    return r
                pl.pe(mm, waits=[tw])

                def ev(e, tc=tc, tt=tt, bo=bo):
                    r = None
                    for g in range(4):
                        r = e.activation(out=U[:, tt * 4 + tc, g * 512:(g + 1) * 512], in_=ps[bo + g][:, :], func=AF.Copy)
                    return r
                pl.evac("act", ev)
                pl.next()
            pl.end()
            xsl.release(P)
        cp = 0
        for tt in range(NT):
            tx = P.load("sp", xsl, [(xt3, tile_view(xs, tt))], dram_names=["xs"])
            pl = Pipe(P)
            for g in range(4):
                sl = pm[cp % 2]
                cp += 1
                wv = (2, 4, 8, 16)[g]
                blo = (wv // 2) * 65
                bhi = (wv - wv // 2 - 1) * 65
                jlo = max(0, (tt * TT - blo) // 128)
                jhi = min(15, (tt * TT + TT - 1 + bhi) // 128)
                nj = jhi - jlo + 1
                tok = P.load("sp", sl, [(sl.t[:, 0:nj, :], poolP[g].rearrange("(j p) t -> p j t", p=128)[:, jlo:jhi + 1, tt * TT:(tt + 1) * TT])])
                bo = 4 * pl.parity()

                def mm(e, sl=sl, g=g, bo=bo, jlo=jlo, nj=nj):
                    r = None
                    for dc in range(4):
                        for jj in range(nj):
                            r = e.matmul(ps[bo + dc][:, :], U[:, jlo + jj, g * 512 + dc * 128:g * 512 + (dc + 1) * 128], sl.t[:, jj, :],
                                         start=(jj == 0), stop=(jj == nj - 1))
                    return r
                pl.pe(mm, waits=[tok])
                sl.release(P)

                def ev(e, g=g, bo=bo):
                    r = None
                    for dc in range(4):
                        c = g * 4 + dc
                        r = e.activation(out=o[:, c, :], in_=ps[bo + dc][:, :], func=AF.Identity,
                                         scale=psc.t[:, pl_ * KC + c:pl_ * KC + c + 1])
                    return r
                pl.evac("act", ev, waits=[tsc])
                pl.next()
            pl.end()
            postnorm_add(xt3, o3, sq, rstd[:, :], lnv[:, :], li, j, waits=[tx])
            P.store("sp", "xt", [(tile_view(xs, tt), xt3)], "xs")
            xsl.release(P)
        P.barrier()

    SEG = 256
    NSEG = T // SEG

    def seg_view(ap, s0, w):
        return ap.rearrange("(k p) t -> p k t", p=128)[:, :, s0:s0 + w]

    def stage_conv(li):
        j = 1
        HAL = 15
        xsl = Slot(P, "xt", [128, KC, TT], F32)
        o = P.sb("o", [128, KC, TT], F32)
        h = P.sb("h", [128, KC, TT], BF16)
        sq = P.sb("sq", [128, KC, TT], BF16)
        gl = P.sb("gl", [128, KC, TT], BF16)
        rstd = P.sb("rstd", [128, TT], F32)
        lnv = P.sb("lnv", [128, TT], F32)
        sgs = [P.sb("sg%d" % i, [128, TT], F32) for i in range(2)]
        zt = P.sb("zt", [128, KC, HAL], BF16)
        wsl = [Slot(P, "pw1_%d" % i, [128, 2, KC, 128], BF16) for i in range(3)]
        xt3 = xsl.t[:, :, :]
        P.chain("dve", lambda e: e.memset(zt[:, :, :], 0.0))
        P.store("sp", "zt", [(seg_view(gls, 0, HAL), zt[:, :, :]), (seg_view(gls, HAL + T, HAL), zt[:, :, :])], "gls")
        cw = 0
        for tt in range(NT):
            tx = P.load("sp", xsl, [(xt3, tile_view(xs, tt))], dram_names=["xs"])
            prenorm(xt3, o[:, :, :], h[:, :, :], sq[:, :, :], rstd[:, :], lnv[:, :], li, j, waits=[tx])
            xsl.release(P)
            P.chain("sp", lambda e: e.nop(), wbufs=["gl"])
            pl = Pipe(P)
            for vc in range(KC):
                sl = wsl[cw % 3]
                cw += 1
                tok = P.load("pool", sl, [(sl.t[:, :, :, :], pw1[vc].rearrange("two p (k j) -> p two k j", j=128))])
                pp = pl.parity()
                b0, b1 = ps[2 * pp], ps[2 * pp + 1]
                sgp = sgs[pp]

                def mm(e, sl=sl, b0=b0, b1=b1):
                    r = None
                    for gu, bank in ((0, b0), (1, b1)):
                        for k in range(KC):
                            r = e.matmul(bank[:, :], sl.t[:, gu, k, :], h[:, k, :], start=(k == 0), stop=(k == KC - 1))
                    return r
                pl.pe(mm, waits=[tok])
                sl.release(P)
                pl.evac("act", lambda e, sgp=sgp, b1=b1: e.activation(out=sgp[:, :], in_=b1[:, :], func=AF.Sigmoid))
                pl.evac("dve", lambda e, vc=vc, sgp=sgp, b0=b0: e.tensor_tensor(gl[:, vc, :], sgp[:, :], b0[:, :], op=ALU.mult))
                pl.next()
            pl.end()
            P.store("sp", "gl", [(seg_view(gls, HAL + tt * TT, TT), gl[:, :, :])], "gls")
        P.barrier()
        W = SEG
        xs2 = Slot(P, "xt", [128, KC, W], F32)
        gp = Slot(P, "gp", [128, KC, W + 2 * HAL], BF16)
        cmv = Slot(P, "cmv", [128, 3 * KC], F32)
        cmk = Slot(P, "cmk", [128, 1], F32)
        cv = P.sb("cv", [128, KC, W], F32)
        ub = P.sb("ub", [128, KC, W], BF16)
        sq2 = P.sb("sq2", [128, KC, W], BF16)
        hs = P.sb("hs", [128, KC, W], BF16)
        o2 = P.sb("o2", [128, KC, W], F32)
        mu = P.sb("mu", [128, W], F32)
        var = P.sb("var", [128, W], F32)
        rstd2 = P.sb("rstd2", [128, W], F32)
        lnv2 = P.sb("lnv2", [128, W], F32)
        dgs = [Slot(P, "dg%d" % i, [128, 31, 128], BF16) for i in range(3)]
        w2s = [Slot(P, "pw2_%d" % i, [128, KC, 128], BF16) for i in range(3)]
        tcv = P.load("sp", cmv, [(cmv.t[:, :], cm_vecT)])
        tck = P.load("sp", cmk, [(cmk.t[:, :], cmask)])
        x3 = xs2.t[:, :, :]
        cd = 0
        c2 = 0
        for s in range(NSEG):
            tg = P.load("sp", gp, [(gp.t[:, :, :], seg_view(gls, s * SEG, W + 2 * HAL))], dram_names=["gls"])
            tx = P.load("sp", xs2, [(x3, seg_view(xs, s * SEG, W))], dram_names=["xs"])

            def msk(e):
                e.tensor_scalar(gp.t[:, :, 0:HAL], gp.t[:, :, 0:HAL], cmk.t[:, 0:1], None, op0=ALU.mult)
                return e.tensor_scalar(gp.t[:, :, W + HAL:W + 2 * HAL], gp.t[:, :, W + HAL:W + 2 * HAL], cmk.t[:, 0:1], None, op0=ALU.mult)
            P.chain("dve", msk, waits=[tg, tck])
            pl = Pipe(P)
            for dch in range(KC):
                sl = dgs[cd % 3]
                cd += 1
                tok = P.load("pool", sl, [(sl.t[:, :, :], cm_diag[dch].rearrange("p (k j) -> p k j", j=128))])
                bk = ps[pl.parity()]

                def mm(e, sl=sl, dch=dch, bk=bk):
                    r = None
                    for k in range(31):
                        r = e.matmul(bk[:, 0:W], sl.t[:, k, :], gp.t[:, dch, k:k + W], start=(k == 0), stop=(k == 30))
                    return r
                pl.pe(mm, waits=[tok])
                sl.release(P)
                pl.evac("act", lambda e, dch=dch, bk=bk: e.activation(out=cv[:, dch, :], in_=bk[:, 0:W], func=AF.Identity,
                                                                       bias=cmv.t[:, dch:dch + 1]), waits=[tcv])
                pl.next()
            pl.end()
            gp.release(P)
            P.chain("act", lambda e: e.activation(out=ub[:, :, :], in_=cv[:, :, :], func=AF.Copy))
            P.chain("act", lambda e: e.activation(out=sq2[:, :, :], in_=cv[:, :, :], func=AF.Square))

            def mm(e):
                r = None
                for k in range(KC):
                    e.matmul(ps[1][:, 0:W], ones_bf[:, :], ub[:, k, :], start=(k == 0), stop=(k == KC - 1))
                for k in range(KC):
                    r = e.matmul(ps[2][:, 0:W], ones_bf[:, :], sq2[:, k, :], start=(k == 0), stop=(k == KC - 1))
                return r
            P.chain("pe", mm)
            P.chain("act", lambda e: e.activation(out=mu[:, :], in_=ps[1][:, 0:W], func=AF.Copy, scale=1.0 / D))
            P.chain("dve", lambda e: e.tensor_tensor(var[:, :], mu[:, :], mu[:, :], op=ALU.mult))
            P.chain("dve", lambda e: e.scalar_tensor_tensor(out=var[:, :], in0=ps[2][:, 0:W], scalar=1.0 / D, in1=var[:, :],
                                                            op0=ALU.mult, op1=ALU.subtract))
            P.chain("act", lambda e: e.activation(out=lnv2[:, :], in_=var[:, :], func=AF.Ln, bias=EPS))
            P.chain("act", lambda e: e.activation(out=rstd2[:, :], in_=lnv2[:, :], func=AF.Exp, scale=-0.5))
            P.chain("dve", lambda e: e.tensor_tensor(cv[:, :, :], cv[:, :, :], bc_mid(mu[:, :], KC), op=ALU.subtract))
            P.chain("dve", lambda e: e.tensor_tensor(cv[:, :, :], cv[:, :, :], bc_mid(rstd2[:, :], KC), op=ALU.mult))

            def act_silu(e):
                r = None
                for k in range(KC):
                    r = e.activation(out=hs[:, k, :], in_=cv[:, k, :], func=AF.Silu,
                                     scale=cmv.t[:, KC + k:KC + k + 1], bias=cmv.t[:, 2 * KC + k:2 * KC + k + 1])
                return r
            P.chain("act", act_silu)
            pl = Pipe(P)
            for dc in range(KC):
                sl = w2s[c2 % 3]
                c2 += 1
                tok = P.load("pool", sl, [(sl.t[:, :, :], pw2[dc].rearrange("p (k j) -> p k j", j=128))])
                bk = ps[3 + pl.parity()]

                def mm(e, sl=sl, bk=bk):
                    r = None
                    for k in range(KC):
                        r = e.matmul(bk[:, 0:W], sl.t[:, k, :], hs[:, k, :], start=(k == 0), stop=(k == KC - 1))
                    return r
                pl.pe(mm, waits=[tok])
                sl.release(P)
                pl.evac("act", lambda e, dc=dc, bk=bk: e.activation(out=o2[:, dc, :], in_=bk[:, 0:W], func=AF.Copy))
                pl.next()
            pl.end()
            postnorm_add(x3, o2[:, :, :], sq2[:, :, :], rstd2[:, :], lnv2[:, :], li, j, waits=[tx])
            P.store("sp", "xt", [(seg_view(xs, s * SEG, W), x3)], "xs")
            xs2.release(P)
        P.barrier()

    DI = 4096
    NH = 64
    psall = P.psall
    psA = psall[:, 512:1536]
    psA3 = psA.rearrange("p (r l) -> p r l", l=128)

    def cview(ap, c0, n):
        return ap.rearrange("(c p) f -> p c f", p=128)[:, c0:c0 + n, :]

    def stage_ssd(li):
        j = 1
        xsl = Slot(P, "xt", [128, KC, TT], F32)
        off = P.arena_off
        zbuf = P.sb("zbuf", [128, 32, TT], BF16)
        o = P._alloc("o", [128, KC, TT], F32, off)
        h = P.sb("h", [128, KC, TT], BF16)
        xbuf = P.sb("xbuf", [128, 48, TT], BF16)
        sq = P.sb("sq", [128, KC, TT], BF16)
        rstd = P.sb("rstd", [128, TT], F32)
        lnv = P.sb("lnv", [128, TT], F32)
        wsl = [Slot(P, "win%d" % i, [128, KC, 128], BF16) for i in range(3)]
        wdt = Slot(P, "wdt", [128, KC, 128], BF16)
        dtb = Slot(P, "dtb", [128, 128], F32)
        dtk = P.sb("dtk", [128, 4, 128], F32)
        v1 = P.sb("v1", [128, 128], F32)
        v2 = P.sb("v2", [128, 128], F32)
        v3 = P.sb("v3", [128, 128], F32)
        zt = P.sb("zt2", [128, 48, 2], BF16)
        xt3 = xsl.t[:, :, :]
        P.chain("dve", lambda e: e.memset(zt[:, :, :], 0.0))
        P.store("sp", "zt2", [(seg_view(xbc_raw_d, 0, 2), zt[:, :, :]), (seg_view(xbc_raw_d, 2 + T, 2), zt[:, :, :])], "xbc_raw_d")
        twd = P.load("pool", wdt, [(wdt.t[:, :, :], ssd_wdt.rearrange("p (k j) -> p k j", j=128))])
        tdb = P.load("sp", dtb, [(dtb.t[:, :], dtb_bc)])
        cw = 0
        for tt in range(NT):
            tx = P.load("sp", xsl, [(xt3, tile_view(xs, tt))], dram_names=["xs"])
            P.chain("sp", lambda e: e.nop(), wbufs=["zbuf"])
            prenorm(xt3, o[:, :, :], h[:, :, :], sq[:, :, :], rstd[:, :], lnv[:, :], li, j, waits=[tx])
            xsl.release(P)
            for tc in range(4):
                def mm(e, tc=tc):
                    r = None
                    for k in range(KC):
                        r = e.matmul(ps[4][:, 0:128], h[:, k, tc * 128:(tc + 1) * 128], wdt.t[:, k, :], start=(k == 0), stop=(k == KC - 1))
                    return r
                P.chain("pe", mm, waits=[twd])
                P.chain("dve", lambda e: e.tensor_tensor(v1[:, :], ps[4][:, 0:128], dtb.t[:, :], op=ALU.add), waits=[tdb])
                P.chain("act", lambda e: e.activation(out=v2[:, :], in_=v1[:, :], func=AF.Abs))
                P.chain("act", lambda e: e.activation(out=v3[:, :], in_=v2[:, :], func=AF.Exp, scale=-1.0))
                P.chain("act", lambda e: e.activation(out=v2[:, :], in_=v3[:, :], func=AF.Ln, bias=1.0))
                P.chain("dve", lambda e, tc=tc: e.scalar_tensor_tensor(out=dtk[:, tc, :], in0=v1[:, :], scalar=0.0, in1=v2[:, :],
                                                                        op0=ALU.max, op1=ALU.add), wbufs=["dtk"] if tc == 0 else [])
            P.store("sp", "dtk", [(cview(dt_tok_d, tt * 4, 4), dtk[:, :, :])], "dt_tok_d")
            for (f0, f1) in ((0, 32), (32, 80)):
                if f0 == 32:
                    P.chain("sp", lambda e: e.nop(), wbufs=["xbuf"])
                pl = Pipe(P)
                for fc in range(f0, f1):
                    sl = wsl[cw % 3]
                    cw += 1
                    tok = P.load("pool", sl, [(sl.t[:, :, :], ssd_in[fc].rearrange("p (k j) -> p k j", j=128))])
                    bk = ps[pl.parity()]

                    def mm(e, sl=sl, bk=bk):
                        r = None
                        for k in range(KC):
                            r = e.matmul(bk[:, :], sl.t[:, k, :], h[:, k, :], start=(k == 0), stop=(k == KC - 1))
                        return r
                    pl.pe(mm, waits=[tok])
                    sl.release(P)
                    if fc < 32:
                        pl.evac("act", lambda e, fc=fc, bk=bk: e.activation(out=zbuf[:, fc, :], in_=bk[:, :], func=AF.Silu))
                    else:
                        pl.evac("act", lambda e, fc=fc, bk=bk: e.activation(out=xbuf[:, fc - 32, :], in_=bk[:, :], func=AF.Copy))
                    pl.next()
                pl.end()
                if f0 == 0:
                    P.store("sp", "zbuf", [(seg_view(zs_d, tt * TT, TT), zbuf[:, :, :])], "zs_d")
            P.store("sp", "xbuf", [(seg_view(xbc_raw_d, 2 + tt * TT, TT), xbuf[:, :, :])], "xbc_raw_d")
        P.barrier()
        W = SEG
        xr = Slot(P, "xr", [128, 48, W + 4], BF16)
        xc = P.sb("xc", [128, 48, W], BF16)
        xtk = P.sb("xtk", [128, 2, 5120], BF16)
        cbs = Slot(P, "cbs", [128, 48], F32)
        cmk = Slot(P, "cmk", [128, 1], F32)
        idn = Slot(P, "idn", [128, 128], BF16)
        dgs = [Slot(P, "sdg%d" % i, [128, 5, 128], BF16) for i in range(3)]
        tcb = P.load("sp", cbs, [(cbs.t[:, :], ssd_cbT)])
        tck = P.load("sp", cmk, [(cmk.t[:, :], cmask)])
        tid = P.load("pool", idn, [(idn.t[:, :], ident)])
        cd = 0
        for s in range(NSEG):
            tg = P.load("sp", xr, [(xr.t[:, :, :], seg_view(xbc_raw_d, s * SEG, W + 4))], dram_names=["xbc_raw_d"])

            def msk(e):
                e.tensor_scalar(xr.t[:, :, 0:2], xr.t[:, :, 0:2], cmk.t[:, 0:1], None, op0=ALU.mult)
                return e.tensor_scalar(xr.t[:, :, W + 2:W + 4], xr.t[:, :, W + 2:W + 4], cmk.t[:, 0:1], None, op0=ALU.mult)
            P.chain("dve", msk, waits=[tg, tck])
            P.chain("sp", lambda e: e.nop(), wbufs=["xc", "xtk"])
            pl = Pipe(P)
            for ch in range(48):
                sl = dgs[cd % 3]
                cd += 1
                tok = P.load("pool", sl, [(sl.t[:, :, :], ssd_diag[ch].rearrange("p (k j) -> p k j", j=128))])
                bk = ps[pl.parity()]

                def mm(e, sl=sl, ch=ch, bk=bk):
                    r = None
                    for k in range(5):
                        r = e.matmul(bk[:, 0:W], sl.t[:, k, :], xr.t[:, ch, k:k + W], start=(k == 0), stop=(k == 4))
                    return r
                pl.pe(mm, waits=[tok])
                sl.release(P)
                pl.evac("act", lambda e, ch=ch, bk=bk: e.activation(out=xc[:, ch, :], in_=bk[:, 0:W], func=AF.Silu,
                                                                     bias=cbs.t[:, ch:ch + 1]), waits=[tcb])
                pl.next()
            pl.end()
            xr.release(P)
            P.store("sp", "xc", [(seg_view(xcT_d, s * SEG, W), xc[:, :, :])], "xcT_d")
            pl = Pipe(P)
            for half in range(2):
                for b in range(10):
                    bk = ps[2 + pl.parity()]

                    def mm(e, half=half, b=b, bk=bk):
                        r = None
                        for q in range(4):
                            r = e.matmul(bk[:, q * 128:(q + 1) * 128], xc[:, b * 4 + q, half * 128:(half + 1) * 128], idn.t[:, :],
                                         start=True, stop=True)
                        return r
                    pl.pe(mm, waits=[tid])
                    eng = "act" if b % 2 == 0 else "dve"
                    if eng == "act":
                        pl.evac("act", lambda e, half=half, b=b, bk=bk: e.activation(out=xtk[:, half, b * 512:(b + 1) * 512], in_=bk[:, :], func=AF.Copy))
                    else:
                        pl.evac("dve", lambda e, half=half, b=b, bk=bk: e.tensor_copy(xtk[:, half, b * 512:(b + 1) * 512], bk[:, :]))
                    pl.next()
            pl.end()
            P.store("sp", "xtk", [(cview(xtok_d, s * 2, 2), xtk[:, :, :])], "xtok_d")
        P.barrier()
        tri = Slot(P, "tri", [128, 2, 128], BF16)
        trif = Slot(P, "trif", [128, 2, 128], F32)
        abc = Slot(P, "abc", [128, 128], F32)
        smk = Slot(P, "smk", [128, 32], F32)
        ttr = P.load("pool", tri, [(tri.t[:, :, :], tri_in.rearrange("two p l -> p two l"))])
        ttf = P.load("sp", trif, [(trif.t[:, :, :], tri_in.rearrange("two p l -> p two l"))])
        tal = P.load("sp", abc, [(abc.t[:, :], alog_bc)])
        tsm = P.load("sp", smk, [(smk.t[:, :], scanmask)])
        P.chain("act", lambda e: e.activation(out=abc.t[:, :], in_=abc.t[:, :], func=AF.Exp), waits=[tal])
        P.chain("dve", lambda e: e.tensor_scalar(abc.t[:, :], abc.t[:, :], -1.0, None, op0=ALU.mult))

        def mkbufs(dr):
            B = {}
            sfx = "_%d" % dr
            B["stS"] = Slot(P, "st" + sfx, [128, DI], F32)
            B["stbf"] = P.sb("stbf" + sfx, [128, DI], BF16)
            B["xks"] = [Slot(P, "xk%d%s" % (i, sfx), [128, 5120], BF16) for i in range(2)]
            B["dks"] = [Slot(P, "dk%d%s" % (i, sfx), [128, 128], F32) for i in range(2)]
            B["bcs"] = [Slot(P, "bcs%d%s" % (i, sfx), [128, 16, 128], BF16) for i in range(2)]
            for nm, shp, dt in (("dta", [128, 64], F32), ("dhi", [128, 64], BF16), ("dlo", [128, 64], BF16), ("t64", [128, 64], F32),
                                ("acT", [128, 64], F32), ("cbm", [128, 128], F32), ("segt", [128, 8, 128], F32), ("Lm", [128, 8, 128], F32),
                                ("Lmb", [128, 8, 128], BF16), ("Eo", [128, 8, 128], F32), ("Cs", [128, 8, 128], BF16),
                                ("xdt", [128, 8, 64], BF16), ("xdd", [128, 8, 64], BF16), ("w1", [128, 8], F32), ("w2", [128, 8], F32),
                                ("w3", [128, 8], F32), ("dec", [128, 8], F32), ("yst", [128, 32, 128], F32)):
                B[nm] = P.sb(nm + sfx, shp, dt)
            return B

        def scan_stream(dr, B):
            order = list(range(16)) if dr == 0 else list(range(15, -1, -1))
            last = 127 if dr == 0 else 0
            yd = yf_d if dr == 0 else yb_d
            ydn = "yf_d" if dr == 0 else "yb_d"
            pb = 4 * dr
            psA = psall[:, pb * 512:(pb + 2) * 512]
            psA3 = psA.rearrange("p (r l) -> p r l", l=128)
            alast = psall[:, pb * 512 + last:(pb + 2) * 512:128]
            bkc = ps[pb + 2]
            bky = ps[pb + 3]
            stS = B["stS"]
            st = stS.t
            stbf = B["stbf"]
            dta, dhi, dlo, t64, acT, cbm = B["dta"], B["dhi"], B["dlo"], B["t64"], B["acT"], B["cbm"]
            segt, Lm, Lmb, Eo, Cs, xdt, xdd = B["segt"], B["Lm"], B["Lmb"], B["Eo"], B["Cs"], B["xdt"], B["xdd"]
            w1, w2, w3, dec, yst = B["w1"], B["w2"], B["w3"], B["dec"], B["yst"]
            ystn = "yst_%d" % dr
            stn = "st_%d" % dr
            abd = abc.t[:, dr * 64:(dr + 1) * 64]
            trf = trif.t[:, dr, :]
            trm = tri.t[:, dr, :]
            th = P.load("sp", stS, [(st[:, :], h0T[dr])])
            P.chain("act", lambda e: e.activation(out=stbf[:, :], in_=st[:, :], func=AF.Copy), waits=[th])
            yield
            cl = 0
            for c in order:
                xk = B["xks"][cl % 2]
                dk = B["dks"][cl % 2]
                bcs = B["bcs"][cl % 2]
                cl += 1
                t1 = P.load("sp", xk, [(xk.t[:, :], xtok_d[c * 128:(c + 1) * 128, :])], dram_names=["xtok_d"])
                t2 = P.load("sp", dk, [(dk.t[:, :], dt_tok_d[c * 128:(c + 1) * 128, :])], dram_names=["dt_tok_d"])
                t3 = P.load("sp", bcs, [(bcs.t[:, :, :], xcT_d.rearrange("(k p) t -> p k t", p=128)[:, 32:48, c * 128:(c + 1) * 128])],
                            dram_names=["xcT_d"])
                dtd = dk.t[:, dr * 64:(dr + 1) * 64]
                P.chain("dve", lambda e, dtd=dtd: e.tensor_tensor(dta[:, :], dtd, abd, op=ALU.mult), waits=[t2])
                yield
                P.chain("dve", lambda e: e.tensor_copy(dhi[:, :], dta[:, :]))
                yield
                P.chain("dve", lambda e: e.tensor_tensor(t64[:, :], dta[:, :], dhi[:, :], op=ALU.subtract))
                yield
                P.chain("dve", lambda e: e.tensor_copy(dlo[:, :], t64[:, :]))
                yield

                def mm(e):
                    e.matmul(bkc[:, 128:192], trm, dhi[:, :], start=True, stop=False)
                    return e.matmul(bkc[:, 128:192], trm, dlo[:, :], start=False, stop=True)
                P.chain("pe", mm, waits=[ttr])
                yield
                P.chain("act", lambda e: e.activation(out=acT[:, :], in_=bkc[:, 128:192], func=AF.Copy))
                yield
                for g in range(8):
                    hs0 = 8 * g

                    def mm(e, g=g, hs0=hs0, bcs=bcs):
                        e.matmul(bkc[:, 0:128], bcs.t[:, g, :], bcs.t[:, 8 + g, :], start=True, stop=True)
                        r = None
                        for r_ in range(8):
                            hh = hs0 + r_
                            e.matmul(psA[:, r_ * 128:(r_ + 1) * 128], dhi[:, hh:hh + 1].to_broadcast([128, 128]), trm, start=True, stop=False)
                            r = e.matmul(psA[:, r_ * 128:(r_ + 1) * 128], dlo[:, hh:hh + 1].to_broadcast([128, 128]), trm, start=False, stop=True)
                        return r
                    P.chain("pe", mm, waits=[t3])
                    yield
                    aT8 = acT[:, hs0:hs0 + 8]
                    dt8 = dk.t[:, dr * 64 + hs0:dr * 64 + hs0 + 8]
                    P.chain("dve", lambda e: e.tensor_tensor(cbm[:, :], bkc[:, 0:128], trf, op=ALU.mult), waits=[ttf])
                    yield
                    P.chain("dve", lambda e, aT8=aT8: e.tensor_tensor(segt[:, :, :], psA3, aT8.unsqueeze(2).to_broadcast([128, 8, 128]), op=ALU.subtract))
                    yield
                    P.chain("dve", lambda e: e.tensor_scalar(segt[:, :, :], segt[:, :, :], 0.0, None, op0=ALU.min))
                    yield
                    P.chain("act", lambda e: e.activation(out=Lm[:, :, :], in_=segt[:, :, :], func=AF.Exp))
                    yield
                    P.chain("dve", lambda e: e.tensor_tensor(Lmb[:, :, :], Lm[:, :, :], cbm[:, :].unsqueeze(1).to_broadcast([128, 8, 128]), op=ALU.mult))
                    yield
                    P.chain("act", lambda e: e.activation(out=Eo[:, :, :], in_=psA3, func=AF.Exp))
                    yield
                    P.chain("dve", lambda e, g=g, bcs=bcs: e.tensor_tensor(Cs[:, :, :], Eo[:, :, :],
                                                                            bcs.t[:, 8 + g, :].unsqueeze(1).to_broadcast([128, 8, 128]), op=ALU.mult))
                    yield
                    xg = xk.t[:, g * 512:(g + 1) * 512].rearrange("p (r q) -> p r q", q=64)
                    P.chain("dve", lambda e, xg=xg, dt8=dt8: e.tensor_tensor(xdt[:, :, :], xg, dt8.unsqueeze(2).to_broadcast([128, 8, 64]), op=ALU.mult),
                            waits=[t1])
                    yield

                    def mm(e, hs0=hs0):
                        r = None
                        for r_ in range(8):
                            pr, hf = r_ // 2, r_ % 2
                            out = bky[hf * 64:(hf + 1) * 64, pr * 128:(pr + 1) * 128]
                            hh = hs0 + r_
                            e.matmul(out, xdt[:, r_, :], Lmb[:, r_, :], start=True, stop=False)
                            r = e.matmul(out, stbf[:, hh * 64:(hh + 1) * 64], Cs[:, r_, :], start=False, stop=True)
                        return r
                    P.chain("pe", mm)
                    yield
                    P.chain("act", lambda e, g=g: e.activation(out=yst[:, 4 * g:4 * g + 4, :], in_=bky[:, :].rearrange("p (a l) -> p a l", l=128), func=AF.Copy),
                            wbufs=[ystn] if g == 0 else [])
                    yield
                    P.chain("dve", lambda e, aT8=aT8: e.tensor_tensor(w1[:, :], alast, aT8, op=ALU.subtract))
                    yield
                    P.chain("act", lambda e: e.activation(out=w2[:, :], in_=w1[:, :], func=AF.Exp))
                    yield
                    P.chain("dve", lambda e, dt8=dt8: e.tensor_tensor(w3[:, :], w2[:, :], dt8, op=ALU.mult))
                    yield
                    P.chain("dve", lambda e, xg=xg: e.tensor_tensor(xdd[:, :, :], xg, w3[:, :].unsqueeze(2).to_broadcast([128, 8, 64]), op=ALU.mult))
                    yield
                    P.chain("act", lambda e: e.activation(out=dec[:, :], in_=alast, func=AF.Exp))
                    yield
                    P.chain("pe", lambda e, g=g, xk=xk: e.matmul(bky[:, :], xk.t[:, DI + g * 128:DI + (g + 1) * 128],
                                                                  xdd[:, :, :].rearrange("p r q -> p (r q)"), start=True, stop=True))
                    yield
                    stg = st[:, g * 512:(g + 1) * 512]
                    stg3 = stg.rearrange("p (r q) -> p r q", q=64)
                    P.chain("dve", lambda e, stg3=stg3: e.tensor_tensor(stg3, stg3, dec[:, :].unsqueeze(2).to_broadcast([128, 8, 64]), op=ALU.mult),
                            wbufs=[stn] if g == 0 else [])
                    yield
                    P.chain("dve", lambda e, stg=stg: e.tensor_tensor(stg, stg, bky[:, :], op=ALU.add))
                    yield
                    P.chain("act", lambda e, g=g, stg=stg: e.activation(out=stbf[:, g * 512:(g + 1) * 512], in_=stg, func=AF.Copy))
                    yield
                xk.release(P)
                dk.release(P)
                bcs.release(P)
                P.store_after("sp", ystn, [(yd.rearrange("(k p) t -> p k t", p=128)[:, :, c * 128:(c + 1) * 128], yst[:, :, :])], ydn, [P.last])
                seg_end = (c % 2 == 1) if dr == 0 else (c % 2 == 0)
                if seg_end:
                    P.store_after("sp", stn, [(st_out[c // 2, dr], st[:, :])], "st_out", [P.last])
                col = dr * 16 + c
                P.chain("dve", lambda e, col=col: e.tensor_scalar(st[:, :], st[:, :], smk.t[:, col:col + 1], None, op0=ALU.mult),
                        waits=[tsm], wbufs=[stn])
                yield
                P.chain("act", lambda e: e.activation(out=stbf[:, :], in_=st[:, :], func=AF.Copy))
                yield
            stS.release(P)

        start = P.last
        bufs = [mkbufs(0), mkbufs(1)]
        gens = [scan_stream(0, bufs[0]), scan_stream(1, bufs[1])]
        lasts = [start, start]
        active = [True, True]
        while any(active):
            for dr in range(2):
                if active[dr]:
                    P.last = lasts[dr]
                    try:
                        next(gens[dr])
                    except StopIteration:
                        active[dr] = False
                    lasts[dr] = P.last
        P.barrier()
        yfS = Slot(P, "yfS", [128, 32, W], F32)
        ybS = Slot(P, "ybS", [128, 32, W], F32)
        xfS = Slot(P, "xfS", [128, 32, W], BF16)
        zzS = Slot(P, "zzS", [128, 32, W], BF16)
        sq4 = P.sb("sq4", [128, 32, W], BF16)
        ybf = sq4
        x4 = Slot(P, "xt", [128, KC, W], F32)
        o4 = P.sb("o4", [128, KC, W], F32)
        sqn = P.sb("sqn", [128, KC, W], BF16)
        rs4 = P.sb("rs4", [128, W], F32)
        ln4 = P.sb("ln4", [128, W], F32)
        dgw = Slot(P, "dgw", [128, 64], F32)
        wos = [Slot(P, "wo%d" % i, [128, 32, 128], BF16) for i in range(3)]
        tdg = P.load("sp", dgw, [(dgw.t[:, :], ssd_dgT)])
        yf3 = yfS.t[:, :, :]
        x3 = x4.t[:, :, :]
        co = 0
        for s in range(NSEG):
            ta = P.load("sp", yfS, [(yf3, seg_view(yf_d, s * SEG, W))], dram_names=["yf_d"])
            tb = P.load("sp", ybS, [(ybS.t[:, :, :], seg_view(yb_d, s * SEG, W))], dram_names=["yb_d"])
            tcx = P.load("sp", xfS, [(xfS.t[:, :, :], xcT_d.rearrange("(k p) t -> p k t", p=128)[:, 0:32, s * SEG:(s + 1) * SEG])], dram_names=["xcT_d"])
            tz = P.load("sp", zzS, [(zzS.t[:, :, :], seg_view(zs_d, s * SEG, W))], dram_names=["zs_d"])
            tx = P.load("sp", x4, [(x3, seg_view(xs, s * SEG, W))], dram_names=["xs"])
            P.chain("dve", lambda e: e.tensor_tensor(yf3, yf3, ybS.t[:, :, :], op=ALU.add), waits=[ta, tb])

            def dsk(e):
                r = None
                for fc in range(32):
                    r = e.scalar_tensor_tensor(out=yfS.t[:, fc, :], in0=xfS.t[:, fc, :], scalar=dgw.t[:, fc:fc + 1], in1=yfS.t[:, fc, :],
                                               op0=ALU.mult, op1=ALU.add)
                return r
            P.chain("dve", dsk, waits=[tcx, tdg])
            P.chain("dve", lambda e: e.tensor_tensor(yf3, yf3, zzS.t[:, :, :], op=ALU.mult), waits=[tz])
            rms_stats(yf3, sq4[:, :, :], rs4[:, :], ln4[:, :], DI)
            P.chain("dve", lambda e: e.tensor_tensor(yf3, yf3, bc_mid(rs4[:, :], 32), op=ALU.mult))

            def gsc(e):
                r = None
                for fc in range(32):
                    r = e.tensor_scalar(ybf[:, fc, :], yfS.t[:, fc, :], dgw.t[:, 32 + fc:33 + fc], None, op0=ALU.mult)
                return r
            P.chain("dve", gsc)
            ybS.release(P)
            xfS.release(P)
            zzS.release(P)
            pl = Pipe(P)
            for dc in range(KC):
                sl = wos[co % 3]
                co += 1
                tok = P.load("pool", sl, [(sl.t[:, :, :], ssd_out[dc].rearrange("p (k j) -> p k j", j=128))])
                bk = ps[3 + pl.parity()]

                def mm(e, sl=sl, bk=bk):
                    r = None
                    for k in range(32):
                        r = e.matmul(bk[:, 0:W], sl.t[:, k, :], ybf[:, k, :], start=(k == 0), stop=(k == 31))
                    return r
                pl.pe(mm, waits=[tok])
                sl.release(P)
                pl.evac("act", lambda e, dc=dc, bk=bk: e.activation(out=o4[:, dc, :], in_=bk[:, 0:W], func=AF.Copy))
                pl.next()
            pl.end()
            yfS.release(P)
            postnorm_add(x3, o4[:, :, :], sqn[:, :, :], rs4[:, :], ln4[:, :], li, j, waits=[tx])
            P.store("sp", "xt", [(seg_view(xs, s * SEG, W), x3)], "xs")
            x4.release(P)
        P.barrier()

    names = {"in": (xT_in, "xT"), "xs": (xs, "xs"), "out": (yT, "yT")}
    for st in stages:
        kind = st[0]
        if kind == "adaln":
            stage_adaln()
        elif kind == "pool":
            stage_pool(st[1])
        elif kind == "conv":
            stage_conv(st[1])
        elif kind == "ssd":
            stage_ssd(st[1])
        elif kind == "ffn":
            _, li, j, s, d = st
            stage_ffn2(li, j, names[s][0], names[d][0], names[s][1], names[d][1])
        else:
            raise ValueError(kind)
    P.final_wait()

    with nc.Block() as block:
        @block.tensor
        def _(e):
            for f in P.ops["pe"]:
                f(e)

        @block.scalar
        def _(e):
            for f in P.ops["act"]:
                f(e)

        @block.vector
        def _(e):
            for f in P.ops["dve"]:
                f(e)

        @block.sync
        def _(e):
            for f in P.ops["sp"]:
                f(e)

        @block.gpsimd
        def _(e):
            for f in P.ops["pool"]:
                f(e)
    es.close()
    return nc


def _pool_mats(prompt):
    def win1d(n, w):
        m = np.zeros((n, n), np.float64)
        for pos in range(n):
            lo = min(max(pos - w // 2, 0), n)
            hi = min(max(pos + (w - w // 2), 0), n)
            m[pos, lo:hi] = 1.0 / (hi - lo)
        return m
    out = np.zeros((4, T, T), np.float32)
    for g, w in enumerate((2, 4, 8, 16)):
        if prompt:
            m1 = win1d(256, w)
            M = np.kron(np.eye(8), m1)
        else:
            M = np.kron(win1d(32, w), win1d(64, w))
        M = M - np.eye(T)
        out[g] = M.T
    return out.astype(ml_dtypes.bfloat16)


def prep_shared(inp):
    f = np.float32
    sh = {}
    sh["ada_w"] = np.ascontiguousarray(inp["ada_w"], dtype=f)
    ab = np.asarray(inp["ada_b"], f).reshape(DEPTH, 9, KC, 128)
    sh["ada_bT"] = np.ascontiguousarray(ab.transpose(3, 0, 1, 2).reshape(128, -1))
    nw = np.asarray(inp["norm_w"], f).reshape(DEPTH, 3, 2, KC, 128)
    sh["nwT"] = np.ascontiguousarray(nw.transpose(4, 0, 1, 2, 3).reshape(128, -1))

    def colblk(w, nk):
        ncol = w.shape[1]
        return np.ascontiguousarray(w.reshape(nk, 128, ncol // 128, 128).transpose(2, 1, 0, 3).reshape(ncol // 128, 128, nk * 128))
    for nm, key in (("wg", "ffn_wg"), ("wu", "ffn_wu")):
        w = np.asarray(inp[key], f).reshape(8, D, FF)
        sh[nm] = np.stack([colblk(w[i], KC) for i in range(8)])
    w = np.asarray(inp["ffn_wd"], f).reshape(8, FF, D)
    sh["wd"] = np.stack([colblk(w[i], FC) for i in range(8)])
    pw = np.asarray(inp["pool_w"], f).reshape(2, 4, 4, 128, 512)
    sh["pool_wT"] = np.ascontiguousarray(pw.transpose(0, 3, 1, 2, 4).reshape(2, 128, -1))
    psc = np.asarray(inp["pool_scale"], f).reshape(2, KC, 128)
    sh["pool_scT"] = np.ascontiguousarray(psc.transpose(2, 0, 1).reshape(128, -1))
    p1 = colblk(np.asarray(inp["cm_pw1"], f)[0], KC)
    sh["pw1"] = np.ascontiguousarray(np.stack([p1[:KC], p1[KC:]], axis=1))
    dw = np.asarray(inp["cm_dw_w"], f)[0]
    dg = np.zeros((KC, 128, 31, 128), f)
    ar = np.arange(128)
    for c in range(KC):
        dg[c, ar, :, ar] = dw[:, c * 128:(c + 1) * 128].T
    sh["cm_diag"] = dg.reshape(KC, 128, 31 * 128)
    vec = np.stack([np.asarray(inp[k], f)[0] for k in ("cm_dw_b", "cm_ln_w", "cm_ln_b")])
    sh["cm_vecT"] = np.ascontiguousarray(vec.reshape(3, KC, 128).transpose(2, 0, 1).reshape(128, -1))
    sh["pw2"] = colblk(np.asarray(inp["cm_pw2"], f)[0], KC)
    inw = np.asarray(inp["ssd_in_w"], f)[0]
    sh["ssd_in"] = colblk(inw[:, :10240], KC)
    sh["ssd_wdt"] = np.ascontiguousarray(inw[:, 10240:].reshape(KC, 128, 128).transpose(1, 0, 2).reshape(128, -1))
    cw = np.asarray(inp["ssd_conv_w"], f)[0]
    dg = np.zeros((48, 128, 5, 128), f)
    for c in range(48):
        dg[c, ar, :, ar] = cw[:, c * 128:(c + 1) * 128].T
    sh["ssd_diag"] = dg.reshape(48, 128, 5 * 128)
    sh["ssd_cbT"] = np.ascontiguousarray(np.asarray(inp["ssd_conv_b"], f)[0].reshape(48, 128).T)
    sh["dtb_bc"] = np.ascontiguousarray(np.broadcast_to(np.asarray(inp["ssd_dt_bias"], f)[0].reshape(1, 128), (128, 128)))
    sh["alog_bc"] = np.ascontiguousarray(np.broadcast_to(np.asarray(inp["ssd_a_log"], f)[0].reshape(1, 128), (128, 128)))
    dfeat = np.repeat(np.asarray(inp["ssd_d"], f)[0], 64)
    gw = np.asarray(inp["ssd_norm_w"], f)[0]
    sh["ssd_dgT"] = np.ascontiguousarray(np.concatenate([dfeat.reshape(32, 128).T, gw.reshape(32, 128).T], axis=1))
    sh["ssd_out"] = colblk(np.asarray(inp["ssd_out_w"], f)[0], 32)
    sh["ident"] = np.eye(128, dtype=f)
    tu = np.triu(np.ones((128, 128), f))
    sh["tri_in"] = np.ascontiguousarray(np.stack([tu, tu.T]))
    return sh


_POOLP = {}


def core_inputs(inp, core):
    f = np.float32
    m = {}
    prompt = core < 4
    if prompt:
        x = np.asarray(inp["x_prompt"], f)[core * 8:(core + 1) * 8].reshape(T, D)
        c = np.asarray(inp["c_ctx"], f)
        h0 = np.zeros((2, 128, 4096), f)
    else:
        x = np.asarray(inp["x_sample"], f)[core - 4].reshape(T, D)
        c = np.asarray(inp["c"], f)[core - 4]
        s0 = np.asarray(inp["state_ssd"], f)[core - 4, 0]
        h0 = np.ascontiguousarray(s0.reshape(2, 4096, 128).transpose(0, 2, 1))
    m["xT"] = np.ascontiguousarray(x.T)
    m["cvec"] = np.ascontiguousarray(c.reshape(KC, 128).T)
    m["h0T"] = h0
    m["cmask"] = np.full((128, 1), 0.0 if prompt else 1.0, f)
    sm = np.ones((2, 16), f)
    if prompt:
        sm[0, 1::2] = 0.0
        sm[1, 0::2] = 0.0
    m["scanmask"] = np.ascontiguousarray(np.broadcast_to(sm.reshape(1, 32), (128, 32)))
    if prompt not in _POOLP:
        _POOLP[prompt] = _pool_mats(prompt)
    m["poolP"] = _POOLP[prompt]
    return m


def full_stages():
    st = [("adaln",)]
    for li in range(DEPTH):
        st.append(("ffn", li, 0, "in" if li == 0 else "xs", "xs"))
        st.append((("pool", "ssd", "conv")[li % 3], li))
        st.append(("ffn", li, 2, "xs", "out" if li == DEPTH - 1 else "xs"))
    return st


def kernel(**inp):
    sh = prep_shared(inp)
    in_maps = []
    for core in range(NCORES):
        m = dict(sh)
        m.update(core_inputs(inp, core))
        in_maps.append(m)
    nc = build_program(full_stages())
    res = run_bass_kernel_spmd(nc, in_maps, core_ids=list(range(NCORES)))
    r = res.results
    yp = np.stack([r[c]["yT"].T for c in range(4)]).reshape(32, 256, D)
    ys = np.stack([r[c]["yT"].T for c in range(4, 8)]).reshape(4, T, D)
    so = np.stack([r[c]["st_out"] for c in range(4)]).reshape(32, 2, 128, 64, 64)
    ns = np.ascontiguousarray(so.transpose(0, 1, 3, 4, 2)).reshape(32, 1, 2, 64, 64, 128)
    return (np.ascontiguousarray(yp, dtype=np.float32), np.ascontiguousarray(ys, dtype=np.float32),
            np.ascontiguousarray(ns, dtype=np.float32))
```

```python
import numpy as np
import ml_dtypes
from contextlib import ExitStack
import concourse.bass as bass
import concourse.mybir as mybir
from concourse.bass_utils import run_bass_kernel_spmd

F32 = mybir.dt.float32
BF16 = mybir.dt.bfloat16
AF = mybir.ActivationFunctionType
ALU = mybir.AluOpType

D = 2048
KC = 16
T = 2048
TT = 512
NT = T // TT
FF = 5632
FC = FF // 128
DEPTH = 4
EPS = 1e-6
NCORES = 8

SB_BASE = 16512
SB_END = 229376
PERSIST = 10 * 1024


class Slot:
    def __init__(self, P, name, shape, dt):
        self.name = name
        self.t = P.sb(name, shape, dt)
        self.sem = P.sem(name + "_s")
        self.P = P
        self.last_use = P.last

    @property
    def cnt(self):
        return self.P.sem_cnt[self.name + "_s"]

    @cnt.setter
    def cnt(self, v):
        self.P.sem_cnt[self.name + "_s"] = v

    def release(self, P):
        self.last_use = P.last


class Prog:
    def __init__(self, nc, es):
        self.nc = nc
        self.es = es
        self.ops = {"pe": [], "act": [], "dve": [], "sp": [], "pool": []}
        self.sem_cache = {}
        self.sem_cnt = {}
        self.esem = {e: self.sem("S_" + e) for e in ("pe", "act", "dve", "sp")}
        self.ecnt = {"pe": 0, "act": 0, "dve": 0, "sp": 0}
        self.last = None
        self.n = 0
        self.buf_store = {}
        self.dram_store = {}
        self.store_sems = {}
        self.store_cnt = {}
        self.uid = 0
        self.persist_off = SB_BASE
        self.arena_off = SB_BASE + PERSIST
        self.psall_t = es.enter_context(nc.psum_tensor("psall", [128, 4096], F32))
        self.psall = self.psall_t[:, :]
        self.psum = [self.psall_t[:, i * 512:(i + 1) * 512] for i in range(8)]

    def sem(self, name):
        if name not in self.sem_cache:
            self.sem_cache[name] = self.es.enter_context(self.nc.semaphore(name))
            self.sem_cnt[name] = 0
        return self.sem_cache[name]

    def _alloc(self, name, shape, dt, off):
        self.uid += 1
        return self.nc.alloc_sbuf_tensor_at("%s_%d" % (name, self.uid), list(shape), dt, offset=off)

    @staticmethod
    def _bytes(shape, dt):
        n = 1
        for s in shape[1:]:
            n *= s
        b = n * (2 if dt == BF16 else 4)
        return (b + 63) // 64 * 64

    def persist(self, name, shape, dt):
        off = self.persist_off
        self.persist_off += self._bytes(shape, dt)
        assert self.persist_off <= SB_BASE + PERSIST, "persist overflow"
        return self._alloc(name, shape, dt, off)

    def sb(self, name, shape, dt):
        off = self.arena_off
        self.arena_off += self._bytes(shape, dt)
        assert self.arena_off <= SB_END, "arena overflow %s %d" % (name, self.arena_off)
        return self._alloc(name, shape, dt, off)

    def chain(self, eng, fn, waits=(), wbufs=(), deps=None):
        idx = self.ecnt[eng]
        self.ecnt[eng] += 1
        d = [self.last] if deps is None else list(deps)
        d = [x for x in d if x is not None]
        ws = list(waits)
        for b in wbufs:
            if b in self.buf_store:
                ws.append(self.buf_store.pop(b))
        esem = self.esem

        def emit(e):
            for (de, di) in d:
                e.wait_ge(esem[de], di + 1)
            for (s, v) in ws:
                e.wait_ge(s, v)
            fn(e).then_inc(esem[eng], 1)

        self.ops[eng].append(emit)
        self.n += 1
        self.last = (eng, idx)
        return self.last

    def join_deps(self):
        return [(e, c - 1) for e, c in self.ecnt.items() if c > 0]

    def load(self, q, slot, pairs, dram_names=()):
        free_after = slot.last_use
        ws = []
        if slot.name in self.buf_store:
            ws.append(self.buf_store.pop(slot.name))
        esem = self.esem
        sem = slot.sem

        def emit(e):
            if free_after is not None:
                e.wait_ge(esem[free_after[0]], free_after[1] + 1)
            for (s, v) in ws:
                e.wait_ge(s, v)
            for dst, src in pairs:
                e.dma_start(out=dst, in_=src).then_inc(sem, 16)

        slot.cnt += 16 * len(pairs)
        self.ops[q].append(emit)
        return (slot.sem, slot.cnt)

    def store(self, q, bufname, pairs, dram_name):
        if bufname not in self.store_sems:
            self.store_sems[bufname] = self.sem("st_" + bufname)
            self.store_cnt[bufname] = 0
        sem = self.store_sems[bufname]
        jd = self.join_deps()
        esem = self.esem

        def emit(e):
            for (de, di) in jd:
                e.wait_ge(esem[de], di + 1)
            for dst, src in pairs:
                e.dma_start(out=dst, in_=src).then_inc(sem, 16)

        self.store_cnt[bufname] += 16 * len(pairs)
        tok = (sem, self.store_cnt[bufname])
        self.buf_store[bufname] = tok
        self.dram_store.setdefault(dram_name, {})[bufname] = tok
        self.ops[q].append(emit)

    def store_after(self, q, bufname, pairs, dram_name, deps):
        if bufname not in self.store_sems:
            self.store_sems[bufname] = self.sem("st_" + bufname)
            self.store_cnt[bufname] = 0
        sem = self.store_sems[bufname]
        jd = [x for x in deps if x is not None]
        esem = self.esem

        def emit(e):
            for (de, di) in jd:
                e.wait_ge(esem[de], di + 1)
            for dst, src in pairs:
                e.dma_start(out=dst, in_=src).then_inc(sem, 16)

        self.store_cnt[bufname] += 16 * len(pairs)
        tok = (sem, self.store_cnt[bufname])
        self.buf_store[bufname] = tok
        self.dram_store.setdefault(dram_name, {})[bufname] = tok
        self.ops[q].append(emit)

    def barrier(self):
        ws = [(s, self.store_cnt[b]) for b, s in self.store_sems.items()]
        self.chain("sp", lambda e: e.nop(), waits=ws, deps=self.join_deps())
        self.buf_store = {}
        self.dram_store = {}
        self.arena_off = SB_BASE + PERSIST

    def final_wait(self):
        ws = [(s, self.store_cnt[b]) for b, s in self.store_sems.items()]
        jd = self.join_deps()
        esem = self.esem

        def emit(e):
            for (de, di) in jd:
                e.wait_ge(esem[de], di + 1)
            for (s, v) in ws:
                e.wait_ge(s, v)

        self.ops["sp"].append(emit)


def bc_mid(ap2d, k):
    f = ap2d.shape[-1]
    return ap2d.unsqueeze(1).to_broadcast([128, k, f])


class Pipe:
    def __init__(self, P):
        self.P = P
        self.i = 0
        self.ev = {}

    def parity(self):
        return self.i % 2

    def pe(self, fn, waits=()):
        P = self.P
        deps = [P.last] if self.i == 0 else []
        if self.i >= 2:
            deps.append(self.ev[self.i - 2])
        self.hpe = P.chain("pe", fn, waits=waits, deps=deps)
        self.cur = self.hpe
        return self.hpe

    def evac(self, eng, fn, waits=(), wbufs=()):
        self.cur = self.P.chain(eng, fn, waits=waits, wbufs=wbufs, deps=[self.cur])
        return self.cur

    def next(self):
        self.ev[self.i] = self.cur
        self.i += 1

    def end(self):
        P = self.P
        P.chain("sp", lambda e: e.nop(), deps=P.join_deps())


def build_program(stages, debug_out=False):
    nc = bass.Bass("TRN2", target_bir_lowering=False)
    es = ExitStack()
    P = Prog(nc, es)

    def din(name, shape, dt=F32):
        return nc.dram_tensor(name, list(shape), dt, kind="ExternalInput").ap()

    xT_in = din("xT", [D, T])
    cvec = din("cvec", [128, KC])
    ada_w = din("ada_w", [DEPTH, D, 9 * D])
    ada_bT = din("ada_bT", [128, DEPTH * 9 * KC])
    nwT_in = din("nwT", [128, DEPTH * 3 * 2 * KC])
    wg = din("wg", [8, FC, 128, KC * 128])
    wu = din("wu", [8, FC, 128, KC * 128])
    wd = din("wd", [8, KC, 128, FC * 128])
    pool_wT = din("pool_wT", [2, 128, 4 * 4 * 512])
    pool_scT = din("pool_scT", [128, 2 * KC])
    poolP = din("poolP", [4, T, T], BF16)
    pw1 = din("pw1", [KC, 2, 128, KC * 128])
    cm_diag = din("cm_diag", [KC, 128, 31 * 128])
    cm_vecT = din("cm_vecT", [128, 3 * KC])
    pw2 = din("pw2", [KC, 128, KC * 128])
    cmask = din("cmask", [128, 1])
    ssd_in = din("ssd_in", [80, 128, KC * 128])
    ssd_wdt = din("ssd_wdt", [128, KC * 128])
    ssd_diag = din("ssd_diag", [48, 128, 5 * 128])
    ssd_cbT = din("ssd_cbT", [128, 48])
    dtb_bc = din("dtb_bc", [128, 128])
    alog_bc = din("alog_bc", [128, 128])
    ssd_dgT = din("ssd_dgT", [128, 64])
    ssd_out = din("ssd_out", [KC, 128, 32 * 128])
    h0T = din("h0T", [2, 128, 4096])
    scanmask = din("scanmask", [128, 32])
    ident = din("ident", [128, 128])
    tri_in = din("tri_in", [2, 128, 128])
    st_out = nc.dram_tensor("st_out", [8, 2, 128, 4096], F32, kind="ExternalOutput").ap()

    def dscr(name, shape, dt):
        return nc.dram_tensor(name, list(shape), dt, kind="Internal").ap()
    gls = dscr("gls", [D, T + 30], BF16)
    zs_d = dscr("zs_d", [4096, T], BF16)
    xbc_raw_d = dscr("xbc_raw_d", [6144, T + 4], BF16)
    dt_tok_d = dscr("dt_tok_d", [T, 128], F32)
    xtok_d = dscr("xtok_d", [T, 5120], BF16)
    xcT_d = dscr("xcT_d", [6144, T], BF16)
    yf_d = dscr("yf_d", [4096, T], F32)
    yb_d = dscr("yb_d", [4096, T], F32)
    yT = nc.dram_tensor("yT", [D, T], F32, kind="ExternalOutput").ap()
    xs = nc.dram_tensor("xs", [D, T], F32, kind="Internal").ap()

    def tile_view(ap, tt):
        return ap.rearrange("(k p) t -> p k t", p=128)[:, :, tt * TT:(tt + 1) * TT]

    ones_bf = P.persist("ones_bf", [128, 128], BF16)
    nwT = P.persist("nwT", [128, DEPTH * 3 * 2 * KC], F32)
    modT = P.persist("modT", [128, DEPTH * 9 * KC], F32)
    coefA = P.persist("coefA", [128, DEPTH * 3 * KC], F32)
    coefC = P.persist("coefC", [128, DEPTH * 3 * KC], F32)
    scb = P.persist("scb", [128, KC], BF16)
    ps = P.psum

    def nw_ap(li, j, w):
        o = ((li * 3 + j) * 2 + w) * KC
        return nwT[:, o:o + KC]

    def mod_ap(li, m):
        o = (li * 9 + m) * KC
        return modT[:, o:o + KC]

    def cf(t, li, j):
        o = (li * 3 + j) * KC
        return t[:, o:o + KC]

    def stage_adaln():
        cslot = Slot(P, "cslot", [128, KC], F32)
        bslot = Slot(P, "bslot", [128, DEPTH * 9 * KC], F32)
        nslot = Slot(P, "nslot", [128, DEPTH * 3 * 2 * KC], F32)
        tc_ = P.load("sp", cslot, [(cslot.t[:, :], cvec)])
        tb = P.load("sp", bslot, [(bslot.t[:, :], ada_bT)])
        tn = P.load("sp", nslot, [(nslot.t[:, :], nwT_in)])
        P.chain("dve", lambda e: e.memset(ones_bf[:, :], 1.0))
        P.chain("act", lambda e: e.activation(out=scb[:, :], in_=cslot.t[:, :], func=AF.Silu), waits=[tc_])
        P.chain("dve", lambda e: e.tensor_copy(nwT[:, :], nslot.t[:, :]), waits=[tn])
        NQ = 6
        QW = 9 * D // NQ
        NB = QW // 512
        wsl = [Slot(P, "adaw%d" % i, [128, QW], BF16) for i in range(4)]
        row = P.sb("modrow", [1, QW], F32)
        rhi = P.sb("rowhi", [1, QW], BF16)
        rlo = P.sb("rowlo", [1, QW], BF16)
        rtmp = P.sb("rowtmp", [1, QW], F32)
        cnt = 0
        for li in range(DEPTH):
            for q in range(NQ):
                for k in range(KC):
                    sl = wsl[cnt % 4]
                    cnt += 1
                    tok = P.load("pool", sl, [(sl.t[:, :], ada_w[li, k * 128:(k + 1) * 128, q * QW:(q + 1) * QW])])

                    def mm(e, sl=sl, k=k):
                        r = None
                        for b in range(NB):
                            r = e.matmul(ps[b][0:1, :], scb[:, k:k + 1], sl.t[:, b * 512:(b + 1) * 512],
                                         start=(k == 0), stop=(k == KC - 1))
                        return r
                    P.chain("pe", mm, waits=[tok])
                    sl.release(P)

                def ev(e):
                    r = None
                    for b in range(NB):
                        r = e.activation(out=row[0:1, b * 512:(b + 1) * 512], in_=ps[b][0:1, :], func=AF.Copy)
                    return r
                P.chain("act", ev)
                P.chain("dve", lambda e: e.tensor_copy(rhi[:, :], row[:, :]))
                P.chain("dve", lambda e: e.tensor_tensor(rtmp[:, :], row[:, :], rhi[:, :], op=ALU.subtract))
                P.chain("dve", lambda e: e.tensor_copy(rlo[:, :], rtmp[:, :]))
                ncol = QW // 128

                def tr(e):
                    r = None
                    for c in range(ncol):
                        e.matmul(ps[6][:, c:c + 1], rhi[0:1, c * 128:(c + 1) * 128], ones_bf[0:1, 0:1],
                                 start=True, stop=False)
                        r = e.matmul(ps[6][:, c:c + 1], rlo[0:1, c * 128:(c + 1) * 128], ones_bf[0:1, 0:1],
                                     start=False, stop=True)
                    return r
                P.chain("pe", tr)
                o = li * 9 * KC + q * ncol
                P.chain("act", lambda e, o=o: e.activation(out=modT[:, o:o + ncol], in_=ps[6][:, 0:ncol], func=AF.Copy))
        P.chain("dve", lambda e: e.tensor_tensor(modT[:, :], modT[:, :], bslot.t[:, :], op=ALU.add), waits=[tb])

        def coefs(e):
            r = None
            for li in range(DEPTH):
                for j in range(3):
                    rw = 1.0 if j == 1 else 0.5
                    e.scalar_tensor_tensor(out=cf(coefA, li, j), in0=mod_ap(li, 3 * j + 1), scalar=1.0,
                                           in1=nw_ap(li, j, 0), op0=ALU.add, op1=ALU.mult)
                    r = e.scalar_tensor_tensor(out=cf(coefC, li, j), in0=mod_ap(li, 3 * j + 2), scalar=rw,
                                               in1=nw_ap(li, j, 1), op0=ALU.mult, op1=ALU.mult)
            return r
        P.chain("dve", coefs)
        P.barrier()

    def rms_stats(src3, sq3, rstd2, lnv2, dim, waits=()):
        nch = src3.shape[1]
        W = src3.shape[2]
        P.chain("act", lambda e: e.activation(out=sq3, in_=src3, func=AF.Square), waits=waits)

        def mm(e):
            r = None
            for k in range(nch):
                r = e.matmul(ps[7][:, 0:W], ones_bf[:, :], sq3[:, k, :], start=(k == 0), stop=(k == nch - 1))
            return r
        P.chain("pe", mm)
        P.chain("act", lambda e: e.activation(out=lnv2, in_=ps[7][:, 0:W], func=AF.Ln, bias=EPS, scale=1.0 / dim))
        P.chain("act", lambda e: e.activation(out=rstd2, in_=lnv2, func=AF.Exp, scale=-0.5))

    def prenorm(xt3, tmp3, h3, sq3, rstd2, lnv2, li, j, waits=()):
        rms_stats(xt3, sq3, rstd2, lnv2, D, waits)
        P.chain("dve", lambda e: e.tensor_tensor(tmp3, xt3, bc_mid(rstd2, KC), op=ALU.mult))
        A = cf(coefA, li, j)
        B = mod_ap(li, 3 * j)

        def aff(e):
            r = None
            for k in range(KC):
                r = e.tensor_scalar(h3[:, k, :], tmp3[:, k, :], A[:, k:k + 1], B[:, k:k + 1], op0=ALU.mult, op1=ALU.add)
            return r
        P.chain("dve", aff)

    def postnorm_add(xt3, o3, sq3, rstd2, lnv2, li, j, waits=()):
        rms_stats(o3, sq3, rstd2, lnv2, D)
        P.chain("dve", lambda e: e.tensor_tensor(o3, o3, bc_mid(rstd2, KC), op=ALU.mult))
        C = cf(coefC, li, j)

        def upd(e):
            r = None
            for k in range(KC):
                r = e.scalar_tensor_tensor(out=xt3[:, k, :], in0=o3[:, k, :], scalar=C[:, k:k + 1], in1=xt3[:, k, :],
                                           op0=ALU.mult, op1=ALU.add)
            return r
        P.chain("dve", upd, waits=waits)

    def stage_ffn(li, j, src, dst, src_name, dst_name):
        fi = 2 * li + (0 if j == 0 else 1)
        xsl = Slot(P, "xt", [128, KC, TT], F32)
        o = P.sb("o", [128, KC, TT], F32)
        h = P.sb("h", [128, KC, TT], BF16)
        act = P.sb("act", [128, FC, TT], BF16)
        sq = act[:, 0:KC, :]
        rstd = P.sb("rstd", [128, TT], F32)
        lnv = P.sb("lnv", [128, TT], F32)
        sg2 = [P.sb("sg%d" % i, [128, TT], F32) for i in range(2)]
        NWGU = 3
        wgu = [Slot(P, "wgu%d" % i, [128, 2, KC, 128], BF16) for i in range(NWGU)]
        wds = [Slot(P, "wd%d" % i, [128, FC, 128], BF16) for i in range(2)]
        xt3 = xsl.t[:, :, :]
        o3 = o[:, :, :]
        cw = 0
        cd = 0
        for tt in range(NT):
            tx = P.load("sp", xsl, [(xt3, tile_view(src, tt))], dram_names=[src_name])
            prenorm(xt3, o3, h[:, :, :], sq, rstd[:, :], lnv[:, :], li, j, waits=[tx])
            dveh = {}
            for f in range(FC):
                sl = wgu[cw % NWGU]
                cw += 1
                tok = P.load("pool", sl, [
                    (sl.t[:, 0, :, :], wg[fi, f].rearrange("p (k j) -> p k j", j=128)),
                    (sl.t[:, 1, :, :], wu[fi, f].rearrange("p (k j) -> p k j", j=128))])
                pp = f % 2
                bg, bu = ps[2 * pp], ps[2 * pp + 1]

                def mm(e, sl=sl, bg=bg, bu=bu):
                    r = None
                    for gu, bank in ((0, bg), (1, bu)):
                        for k in range(KC):
                            r = e.matmul(bank[:, :], sl.t[:, gu, k, :], h[:, k, :], start=(k == 0), stop=(k == KC - 1))
                    return r
                deps = [P.last] if f == 0 else []
                if f >= 2:
                    deps.append(dveh[f - 2])
                hpe = P.chain("pe", mm, waits=[tok], deps=deps)
                sl.release(P)
                sgp = sg2[pp]
                hact = P.chain("act", lambda e, sgp=sgp, bg=bg: e.activation(out=sgp[:, :], in_=bg[:, :], func=AF.Silu), deps=[hpe])
                dveh[f] = P.chain("dve", lambda e, f=f, sgp=sgp, bu=bu: e.tensor_tensor(act[:, f, :], sgp[:, :], bu[:, :], op=ALU.mult), deps=[hact])
            acth = {}
            for dc in range(KC):
                sl = wds[cd % 2]
                cd += 1
                tok = P.load("pool", sl, [(sl.t[:, :, :], wd[fi, dc].rearrange("p (k j) -> p k j", j=128))])
                bo = ps[4 + dc % 2]

                def mm2(e, sl=sl, bo=bo):
                    r = None
                    for fk in range(FC):
                        r = e.matmul(bo[:, :], sl.t[:, fk, :], act[:, fk, :], start=(fk == 0), stop=(fk == FC - 1))
                    return r
                deps = [dveh[FC - 1]] if dc == 0 else []
                if dc >= 2:
                    deps.append(acth[dc - 2])
                hpe = P.chain("pe", mm2, waits=[tok], deps=deps)
                sl.release(P)
                acth[dc] = P.chain("act", lambda e, dc=dc, bo=bo: e.activation(out=o[:, dc, :], in_=bo[:, :], func=AF.Copy), deps=[hpe])
            postnorm_add(xt3, o3, sq, rstd[:, :], lnv[:, :], li, j)
            P.store("sp", "xt", [(tile_view(dst, tt), xt3)], dst_name)
            xsl.release(P)
        P.barrier()

    def stage_ffn2(li, j, src, dst, src_name, dst_name):
        fi = 2 * li + (0 if j == 0 else 1)
        TP = 2 * TT
        FH = FC // 2
        h = P.sb("h", [128, KC, TP], BF16)
        off_act = P.arena_off
        actH = P.sb("actH", [128, FH, TP], BF16)
        off_o = P.arena_off
        o = P.sb("o", [128, KC, TP], F32)
        xA_t = P._alloc("xA", [128, KC, TT], F32, off_o)
        sqA = P._alloc("sqA", [128, KC, TT], BF16, off_o + 32768)
        tmpA = P._alloc("tmpA", [128, KC, TT], F32, off_act)
        xB_t = P._alloc("xB", [128, KC, TT], F32, off_act)
        sqB = h[:, :, 0:TT]
        xA = Slot.__new__(Slot); xA.name = "xA"; xA.t = xA_t; xA.sem = P.sem("xA_s"); xA.P = P; xA.last_use = P.last
        xB = Slot.__new__(Slot); xB.name = "xB"; xB.t = xB_t; xB.sem = P.sem("xB_s"); xB.P = P; xB.last_use = P.last
        rstd = P.sb("rstd", [128, TT], F32)
        lnv = P.sb("lnv", [128, TT], F32)
        sg2 = [P.sb("sg%d" % i, [128, TT], F32) for i in range(2)]
        NWGU = 3
        wgu = [Slot(P, "wgu%d" % i, [128, 2, KC, 128], BF16) for i in range(NWGU)]
        wds = [Slot(P, "wdh%d" % i, [128, FH, 128], BF16) for i in range(2)]
        cw = 0
        cd = 0
        for pr in range(T // TP):
            if pr > 0:
                P.chain("sp", lambda e: e.nop(), wbufs=["xB"])
            for hf in range(2):
                tt = pr * 2 + hf
                tx = P.load("sp", xA, [(xA_t[:, :, :], tile_view(src, tt))], dram_names=[src_name])
                prenorm(xA_t[:, :, :], tmpA[:, :, :], h[:, :, hf * TT:(hf + 1) * TT], sqA[:, :, :], rstd[:, :], lnv[:, :], li, j, waits=[tx])
                xA.release(P)
            hprev = None
            for fh in range(2):
                dveh = {}
                u = 0
                for f in range(FH):
                    fg = fh * FH + f
                    sl = wgu[cw % NWGU]
                    cw += 1
                    tok = P.load("pool", sl, [
                        (sl.t[:, 0, :, :], wg[fi, fg].rearrange("p (k j) -> p k j", j=128)),
                        (sl.t[:, 1, :, :], wu[fi, fg].rearrange("p (k j) -> p k j", j=128))])
                    for hf in range(2):
                        pp = u % 2
                        bg, bu = ps[2 * pp], ps[2 * pp + 1]

                        def mm(e, sl=sl, bg=bg, bu=bu, hf=hf):
                            r = None
                            for gu, bank in ((0, bg), (1, bu)):
                                for k in range(KC):
                                    r = e.matmul(bank[:, :], sl.t[:, gu, k, :], h[:, k, hf * TT:(hf + 1) * TT], start=(k == 0), stop=(k == KC - 1))
                            return r
                        deps = [P.last] if u == 0 else []
                        if u >= 2:
                            deps.append(dveh[u - 2])
                        hpe = P.chain("pe", mm, waits=[tok] if hf == 0 else [], deps=deps)
                        sgp = sg2[pp]
                        hact = P.chain("act", lambda e, sgp=sgp, bg=bg: e.activation(out=sgp[:, :], in_=bg[:, :], func=AF.Silu), deps=[hpe])
                        dveh[u] = P.chain("dve", lambda e, f=f, hf=hf, sgp=sgp, bu=bu: e.tensor_tensor(actH[:, f, hf * TT:(hf + 1) * TT], sgp[:, :], bu[:, :], op=ALU.mult),
                                          deps=[hact])
                        u += 1
                    sl.last_use = hpe
                evh = {}
                u = 0
                for dc in range(KC):
                    sl = wds[cd % 2]
                    cd += 1
                    tok = P.load("pool", sl, [(sl.t[:, :, :], wd[fi, dc][:, fh * FH * 128:(fh + 1) * FH * 128].rearrange("p (k j) -> p k j", j=128))])
                    for hf in range(2):
                        bo = ps[4 + u % 2]

                        def mm2(e, sl=sl, bo=bo, hf=hf):
                            r = None
                            for fk in range(FH):
                                r = e.matmul(bo[:, :], sl.t[:, fk, :], actH[:, fk, hf * TT:(hf + 1) * TT], start=(fk == 0), stop=(fk == FH - 1))
                            return r
                        deps = [dveh[2 * FH - 1]] if u == 0 else []
                        if u >= 2:
                            deps.append(evh[u - 2])
                        hpe = P.chain("pe", mm2, waits=[tok] if hf == 0 else [], deps=deps)
                        osl = o[:, dc, hf * TT:(hf + 1) * TT]
                        if fh == 0:
                            evh[u] = P.chain("act", lambda e, osl=osl, bo=bo: e.activation(out=osl, in_=bo[:, :], func=AF.Copy), deps=[hpe])
                        else:
                            evh[u] = P.chain("dve", lambda e, osl=osl, bo=bo: e.tensor_tensor(osl, osl, bo[:, :], op=ALU.add), deps=[hpe])
                        u += 1
                    sl.last_use = hpe
                P.chain("sp", lambda e: e.nop(), deps=P.join_deps())
            xB.last_use = P.last
            for hf in range(2):
                tt = pr * 2 + hf
                tx = P.load("sp", xB, [(xB_t[:, :, :], tile_view(src, tt))], dram_names=[src_name])
                postnorm_add(xB_t[:, :, :], o[:, :, hf * TT:(hf + 1) * TT], sqB, rstd[:, :], lnv[:, :], li, j, waits=[tx])
                P.store("sp", "xB", [(tile_view(dst, tt), xB_t[:, :, :])], dst_name)
                xB.release(P)
            xA.last_use = P.last
        P.barrier()

    def stage_pool(li):
        pl_ = li // 3
        j = 1
        U = P.sb("U", [128, 16, 2048], BF16)
        Wp = Slot(P, "Wp", [128, 4, 4, 512], BF16)
        psc = Slot(P, "psc", [128, 2 * KC], F32)
        xsl = Slot(P, "xt", [128, KC, TT], F32)
        o = P.sb("o", [128, KC, TT], F32)
        h = P.sb("h", [128, KC, TT], BF16)
        sq = h[:, :, :]
        rstd = P.sb("rstd", [128, TT], F32)
        lnv = P.sb("lnv", [128, TT], F32)
        pm = [Slot(P, "pm%d" % i, [128, 16, TT], BF16) for i in range(2)]
        xt3 = xsl.t[:, :, :]
        o3 = o[:, :, :]
        tw = P.load("pool", Wp, [(Wp.t[:, :, :, :], pool_wT[pl_].rearrange("p (g k d) -> p g k d", g=4, k=4))])
        tsc = P.load("sp", psc, [(psc.t[:, :], pool_scT)])
        for tt in range(NT):
            tx = P.load("sp", xsl, [(xt3, tile_view(xs, tt))], dram_names=["xs"])
            prenorm(xt3, o3, h[:, :, :], sq, rstd[:, :], lnv[:, :], li, j, waits=[tx])
            pl = Pipe(P)
            for tc in range(4):
                bo = 4 * pl.parity()

                def mm(e, tc=tc, bo=bo):
                    r = None
                    for g in range(4):
                        for k in range(4):
                            r = e.matmul(ps[bo + g][:, :], h[:, g * 4 + k, tc * 128:(tc + 1) * 128], Wp.t[:, g, k, :],
                                         start=(k == 0), stop=(k == 3))
                    return r
                pl.pe(mm, waits=[tw])

                def ev(e, tc=tc, tt=tt, bo=bo):
                    r = None
                    for g in range(4):
                        r = e.activation(out=U[:, tt * 4 + tc, g * 512:(g + 1) * 512], in_=ps[bo + g][:, :], func=AF.Copy)
                    return r
                pl.evac("act", ev)
                pl.next()
            pl.end()
            xsl.release(P)
        cp = 0
        for tt in range(NT):
            tx = P.load("sp", xsl, [(xt3, tile_view(xs, tt))], dram_names=["xs"])
            pl = Pipe(P)
            for g in range(4):
                sl = pm[cp % 2]
                cp += 1
                wv = (2, 4, 8, 16)[g]
                blo = (wv // 2) * 65
                bhi = (wv - wv // 2 - 1) * 65
                jlo = max(0, (tt * TT - blo) // 128)
                jhi = min(15, (tt * TT + TT - 1 + bhi) // 128)
                nj = jhi - jlo + 1
                tok = P.load("sp", sl, [(sl.t[:, 0:nj, :], poolP[g].rearrange("(j p) t -> p j t", p=128)[:, jlo:jhi + 1, tt * TT:(tt + 1) * TT])])
                bo = 4 * pl.parity()

                def mm(e, sl=sl, g=g, bo=bo, jlo=jlo, nj=nj):
                    r = None
                    for dc in range(4):
                        for jj in range(nj):
                            r = e.matmul(ps[bo + dc][:, :], U[:, jlo + jj, g * 512 + dc * 128:g * 512 + (dc + 1) * 128], sl.t[:, jj, :],
                                         start=(jj == 0), stop=(jj == nj - 1))
                    return r
                pl.pe(mm, waits=[tok])
                sl.release(P)

                def ev(e, g=g, bo=bo):
                    r = None
                    for dc in range(4):
                        c = g * 4 + dc
                        r = e.activation(out=o[:, c, :], in_=ps[bo + dc][:, :], func=AF.Identity,
                                         scale=psc.t[:, pl_ * KC + c:pl_ * KC + c + 1])
                    return r
                pl.evac("act", ev, waits=[tsc])
                pl.next()
            pl.end()
            postnorm_add(xt3, o3, sq, rstd[:, :], lnv[:, :], li, j, waits=[tx])
            P.store("sp", "xt", [(tile_view(xs, tt), xt3)], "xs")
            xsl.release(P)
        P.barrier()

    SEG = 256
    NSEG = T // SEG

    def seg_view(ap, s0, w):
        return ap.rearrange("(k p) t -> p k t", p=128)[:, :, s0:s0 + w]

    def stage_conv(li):
        j = 1
        HAL = 15
        xsl = Slot(P, "xt", [128, KC, TT], F32)
        o = P.sb("o", [128, KC, TT], F32)
        h = P.sb("h", [128, KC, TT], BF16)
        sq = P.sb("sq", [128, KC, TT], BF16)
        gl = P.sb("gl", [128, KC, TT], BF16)
        rstd = P.sb("rstd", [128, TT], F32)
        lnv = P.sb("lnv", [128, TT], F32)
        sgs = [P.sb("sg%d" % i, [128, TT], F32) for i in range(2)]
        zt = P.sb("zt", [128, KC, HAL], BF16)
        wsl = [Slot(P, "pw1_%d" % i, [128, 2, KC, 128], BF16) for i in range(3)]
        xt3 = xsl.t[:, :, :]
        P.chain("dve", lambda e: e.memset(zt[:, :, :], 0.0))
        P.store("sp", "zt", [(seg_view(gls, 0, HAL), zt[:, :, :]), (seg_view(gls, HAL + T, HAL), zt[:, :, :])], "gls")
        cw = 0
        for tt in range(NT):
            tx = P.load("sp", xsl, [(xt3, tile_view(xs, tt))], dram_names=["xs"])
            prenorm(xt3, o[:, :, :], h[:, :, :], sq[:, :, :], rstd[:, :], lnv[:, :], li, j, waits=[tx])
            xsl.release(P)
            P.chain("sp", lambda e: e.nop(), wbufs=["gl"])
            pl = Pipe(P)
            for vc in range(KC):
                sl = wsl[cw % 3]
                cw += 1
                tok = P.load("pool", sl, [(sl.t[:, :, :, :], pw1[vc].rearrange("two p (k j) -> p two k j", j=128))])
                pp = pl.parity()
                b0, b1 = ps[2 * pp], ps[2 * pp + 1]
                sgp = sgs[pp]

                def mm(e, sl=sl, b0=b0, b1=b1):
                    r = None
                    for gu, bank in ((0, b0), (1, b1)):
                        for k in range(KC):
                            r = e.matmul(bank[:, :], sl.t[:, gu, k, :], h[:, k, :], start=(k == 0), stop=(k == KC - 1))
                    return r
                pl.pe(mm, waits=[tok])
                sl.release(P)
                pl.evac("act", lambda e, sgp=sgp, b1=b1: e.activation(out=sgp[:, :], in_=b1[:, :], func=AF.Sigmoid))
                pl.evac("dve", lambda e, vc=vc, sgp=sgp, b0=b0: e.tensor_tensor(gl[:, vc, :], sgp[:, :], b0[:, :], op=ALU.mult))
                pl.next()
            pl.end()
            P.store("sp", "gl", [(seg_view(gls, HAL + tt * TT, TT), gl[:, :, :])], "gls")
        P.barrier()
        W = SEG
        xs2 = Slot(P, "xt", [128, KC, W], F32)
        gp = Slot(P, "gp", [128, KC, W + 2 * HAL], BF16)
        cmv = Slot(P, "cmv", [128, 3 * KC], F32)
        cmk = Slot(P, "cmk", [128, 1], F32)
        cv = P.sb("cv", [128, KC, W], F32)
        ub = P.sb("ub", [128, KC, W], BF16)
        sq2 = P.sb("sq2", [128, KC, W], BF16)
        hs = P.sb("hs", [128, KC, W], BF16)
        o2 = P.sb("o2", [128, KC, W], F32)
        mu = P.sb("mu", [128, W], F32)
        var = P.sb("var", [128, W], F32)
        rstd2 = P.sb("rstd2", [128, W], F32)
        lnv2 = P.sb("lnv2", [128, W], F32)
        dgs = [Slot(P, "dg%d" % i, [128, 31, 128], BF16) for i in range(3)]
        w2s = [Slot(P, "pw2_%d" % i, [128, KC, 128], BF16) for i in range(3)]
        tcv = P.load("sp", cmv, [(cmv.t[:, :], cm_vecT)])
        tck = P.load("sp", cmk, [(cmk.t[:, :], cmask)])
        x3 = xs2.t[:, :, :]
        cd = 0
        c2 = 0
        for s in range(NSEG):
            tg = P.load("sp", gp, [(gp.t[:, :, :], seg_view(gls, s * SEG, W + 2 * HAL))], dram_names=["gls"])
            tx = P.load("sp", xs2, [(x3, seg_view(xs, s * SEG, W))], dram_names=["xs"])

            def msk(e):
                e.tensor_scalar(gp.t[:, :, 0:HAL], gp.t[:, :, 0:HAL], cmk.t[:, 0:1], None, op0=ALU.mult)
                return e.tensor_scalar(gp.t[:, :, W + HAL:W + 2 * HAL], gp.t[:, :, W + HAL:W + 2 * HAL], cmk.t[:, 0:1], None, op0=ALU.mult)
            P.chain("dve", msk, waits=[tg, tck])
            pl = Pipe(P)
            for dch in range(KC):
                sl = dgs[cd % 3]
                cd += 1
                tok = P.load("pool", sl, [(sl.t[:, :, :], cm_diag[dch].rearrange("p (k j) -> p k j", j=128))])
                bk = ps[pl.parity()]

                def mm(e, sl=sl, dch=dch, bk=bk):
                    r = None
                    for k in range(31):
                        r = e.matmul(bk[:, 0:W], sl.t[:, k, :], gp.t[:, dch, k:k + W], start=(k == 0), stop=(k == 30))
                    return r
                pl.pe(mm, waits=[tok])
                sl.release(P)
                pl.evac("act", lambda e, dch=dch, bk=bk: e.activation(out=cv[:, dch, :], in_=bk[:, 0:W], func=AF.Identity,
                                                                       bias=cmv.t[:, dch:dch + 1]), waits=[tcv])
                pl.next()
            pl.end()
            gp.release(P)
            P.chain("act", lambda e: e.activation(out=ub[:, :, :], in_=cv[:, :, :], func=AF.Copy))
            P.chain("act", lambda e: e.activation(out=sq2[:, :, :], in_=cv[:, :, :], func=AF.Square))

            def mm(e):
                r = None
                for k in range(KC):
                    e.matmul(ps[1][:, 0:W], ones_bf[:, :], ub[:, k, :], start=(k == 0), stop=(k == KC - 1))
                for k in range(KC):
                    r = e.matmul(ps[2][:, 0:W], ones_bf[:, :], sq2[:, k, :], start=(k == 0), stop=(k == KC - 1))
                return r
            P.chain("pe", mm)
            P.chain("act", lambda e: e.activation(out=mu[:, :], in_=ps[1][:, 0:W], func=AF.Copy, scale=1.0 / D))
            P.chain("dve", lambda e: e.tensor_tensor(var[:, :], mu[:, :], mu[:, :], op=ALU.mult))
            P.chain("dve", lambda e: e.scalar_tensor_tensor(out=var[:, :], in0=ps[2][:, 0:W], scalar=1.0 / D, in1=var[:, :],
                                                            op0=ALU.mult, op1=ALU.subtract))
            P.chain("act", lambda e: e.activation(out=lnv2[:, :], in_=var[:, :], func=AF.Ln, bias=EPS))
            P.chain("act", lambda e: e.activation(out=rstd2[:, :], in_=lnv2[:, :], func=AF.Exp, scale=-0.5))
            P.chain("dve", lambda e: e.tensor_tensor(cv[:, :, :], cv[:, :, :], bc_mid(mu[:, :], KC), op=ALU.subtract))
            P.chain("dve", lambda e: e.tensor_tensor(cv[:, :, :], cv[:, :, :], bc_mid(rstd2[:, :], KC), op=ALU.mult))

            def act_silu(e):
                r = None
                for k in range(KC):
                    r = e.activation(out=hs[:, k, :], in_=cv[:, k, :], func=AF.Silu,
                                     scale=cmv.t[:, KC + k:KC + k + 1], bias=cmv.t[:, 2 * KC + k:2 * KC + k + 1])
                return r
            P.chain("act", act_silu)
            pl = Pipe(P)
            for dc in range(KC):
                sl = w2s[c2 % 3]
                c2 += 1
                tok = P.load("pool", sl, [(sl.t[:, :, :], pw2[dc].rearrange("p (k j) -> p k j", j=128))])
                bk = ps[3 + pl.parity()]

                def mm(e, sl=sl, bk=bk):
                    r = None
                    for k in range(KC):
                        r = e.matmul(bk[:, 0:W], sl.t[:, k, :], hs[:, k, :], start=(k == 0), stop=(k == KC - 1))
                    return r
                pl.pe(mm, waits=[tok])
                sl.release(P)
                pl.evac("act", lambda e, dc=dc, bk=bk: e.activation(out=o2[:, dc, :], in_=bk[:, 0:W], func=AF.Copy))
                pl.next()
            pl.end()
            postnorm_add(x3, o2[:, :, :], sq2[:, :, :], rstd2[:, :], lnv2[:, :], li, j, waits=[tx])
            P.store("sp", "xt", [(seg_view(xs, s * SEG, W), x3)], "xs")
            xs2.release(P)
        P.barrier()

    DI = 4096
    NH = 64
    psall = P.psall
    psA = psall[:, 512:1536]
    psA3 = psA.rearrange("p (r l) -> p r l", l=128)

    def cview(ap, c0, n):
        return ap.rearrange("(c p) f -> p c f", p=128)[:, c0:c0 + n, :]

    def stage_ssd(li):
        j = 1
        xsl = Slot(P, "xt", [128, KC, TT], F32)
        off = P.arena_off
        zbuf = P.sb("zbuf", [128, 32, TT], BF16)
        o = P._alloc("o", [128, KC, TT], F32, off)
        h = P.sb("h", [128, KC, TT], BF16)
        xbuf = P.sb("xbuf", [128, 48, TT], BF16)
        sq = P.sb("sq", [128, KC, TT], BF16)
        rstd = P.sb("rstd", [128, TT], F32)
        lnv = P.sb("lnv", [128, TT], F32)
        wsl = [Slot(P, "win%d" % i, [128, KC, 128], BF16) for i in range(3)]
        wdt = Slot(P, "wdt", [128, KC, 128], BF16)
        dtb = Slot(P, "dtb", [128, 128], F32)
        dtk = P.sb("dtk", [128, 4, 128], F32)
        v1 = P.sb("v1", [128, 128], F32)
        v2 = P.sb("v2", [128, 128], F32)
        v3 = P.sb("v3", [128, 128], F32)
        zt = P.sb("zt2", [128, 48, 2], BF16)
        xt3 = xsl.t[:, :, :]
        P.chain("dve", lambda e: e.memset(zt[:, :, :], 0.0))
        P.store("sp", "zt2", [(seg_view(xbc_raw_d, 0, 2), zt[:, :, :]), (seg_view(xbc_raw_d, 2 + T, 2), zt[:, :, :])], "xbc_raw_d")
        twd = P.load("pool", wdt, [(wdt.t[:, :, :], ssd_wdt.rearrange("p (k j) -> p k j", j=128))])
        tdb = P.load("sp", dtb, [(dtb.t[:, :], dtb_bc)])
        cw = 0
        for tt in range(NT):
            tx = P.load("sp", xsl, [(xt3, tile_view(xs, tt))], dram_names=["xs"])
            P.chain("sp", lambda e: e.nop(), wbufs=["zbuf"])
            prenorm(xt3, o[:, :, :], h[:, :, :], sq[:, :, :], rstd[:, :], lnv[:, :], li, j, waits=[tx])
            xsl.release(P)
            for tc in range(4):
                def mm(e, tc=tc):
                    r = None
                    for k in range(KC):
                        r = e.matmul(ps[4][:, 0:128], h[:, k, tc * 128:(tc + 1) * 128], wdt.t[:, k, :], start=(k == 0), stop=(k == KC - 1))
                    return r
                P.chain("pe", mm, waits=[twd])
                P.chain("dve", lambda e: e.tensor_tensor(v1[:, :], ps[4][:, 0:128], dtb.t[:, :], op=ALU.add), waits=[tdb])
                P.chain("act", lambda e: e.activation(out=v2[:, :], in_=v1[:, :], func=AF.Abs))
                P.chain("act", lambda e: e.activation(out=v3[:, :], in_=v2[:, :], func=AF.Exp, scale=-1.0))
                P.chain("act", lambda e: e.activation(out=v2[:, :], in_=v3[:, :], func=AF.Ln, bias=1.0))
                P.chain("dve", lambda e, tc=tc: e.scalar_tensor_tensor(out=dtk[:, tc, :], in0=v1[:, :], scalar=0.0, in1=v2[:, :],
                                                                        op0=ALU.max, op1=ALU.add), wbufs=["dtk"] if tc == 0 else [])
            P.store("sp", "dtk", [(cview(dt_tok_d, tt * 4, 4), dtk[:, :, :])], "dt_tok_d")
            for (f0, f1) in ((0, 32), (32, 80)):
                if f0 == 32:
                    P.chain("sp", lambda e: e.nop(), wbufs=["xbuf"])
                pl = Pipe(P)
                for fc in range(f0, f1):
                    sl = wsl[cw % 3]
                    cw += 1
                    tok = P.load("pool", sl, [(sl.t[:, :, :], ssd_in[fc].rearrange("p (k j) -> p k j", j=128))])
                    bk = ps[pl.parity()]

                    def mm(e, sl=sl, bk=bk):
                        r = None
                        for k in range(KC):
                            r = e.matmul(bk[:, :], sl.t[:, k, :], h[:, k, :], start=(k == 0), stop=(k == KC - 1))
                        return r
                    pl.pe(mm, waits=[tok])
                    sl.release(P)
                    if fc < 32:
                        pl.evac("act", lambda e, fc=fc, bk=bk: e.activation(out=zbuf[:, fc, :], in_=bk[:, :], func=AF.Silu))
                    else:
                        pl.evac("act", lambda e, fc=fc, bk=bk: e.activation(out=xbuf[:, fc - 32, :], in_=bk[:, :], func=AF.Copy))
                    pl.next()
                pl.end()
                if f0 == 0:
                    P.store("sp", "zbuf", [(seg_view(zs_d, tt * TT, TT), zbuf[:, :, :])], "zs_d")
            P.store("sp", "xbuf", [(seg_view(xbc_raw_d, 2 + tt * TT, TT), xbuf[:, :, :])], "xbc_raw_d")
        P.barrier()
        W = SEG
        xr = Slot(P, "xr", [128, 48, W + 4], BF16)
        xc = P.sb("xc", [128, 48, W], BF16)
        xtk = P.sb("xtk", [128, 2, 5120], BF16)
        cbs = Slot(P, "cbs", [128, 48], F32)
        cmk = Slot(P, "cmk", [128, 1], F32)
        idn = Slot(P, "idn", [128, 128], BF16)
        dgs = [Slot(P, "sdg%d" % i, [128, 5, 128], BF16) for i in range(3)]
        tcb = P.load("sp", cbs, [(cbs.t[:, :], ssd_cbT)])
        tck = P.load("sp", cmk, [(cmk.t[:, :], cmask)])
        tid = P.load("pool", idn, [(idn.t[:, :], ident)])
        cd = 0
        for s in range(NSEG):
            tg = P.load("sp", xr, [(xr.t[:, :, :], seg_view(xbc_raw_d, s * SEG, W + 4))], dram_names=["xbc_raw_d"])

            def msk(e):
                e.tensor_scalar(xr.t[:, :, 0:2], xr.t[:, :, 0:2], cmk.t[:, 0:1], None, op0=ALU.mult)
                return e.tensor_scalar(xr.t[:, :, W + 2:W + 4], xr.t[:, :, W + 2:W + 4], cmk.t[:, 0:1], None, op0=ALU.mult)
            P.chain("dve", msk, waits=[tg, tck])
            P.chain("sp", lambda e: e.nop(), wbufs=["xc", "xtk"])
            pl = Pipe(P)
            for ch in range(48):
                sl = dgs[cd % 3]
                cd += 1
                tok = P.load("pool", sl, [(sl.t[:, :, :], ssd_diag[ch].rearrange("p (k j) -> p k j", j=128))])
                bk = ps[pl.parity()]

                def mm(e, sl=sl, ch=ch, bk=bk):
                    r = None
                    for k in range(5):
                        r = e.matmul(bk[:, 0:W], sl.t[:, k, :], xr.t[:, ch, k:k + W], start=(k == 0), stop=(k == 4))
                    return r
                pl.pe(mm, waits=[tok])
                sl.release(P)
                pl.evac("act", lambda e, ch=ch, bk=bk: e.activation(out=xc[:, ch, :], in_=bk[:, 0:W], func=AF.Silu,
                                                                     bias=cbs.t[:, ch:ch + 1]), waits=[tcb])
                pl.next()
            pl.end()
            xr.release(P)
            P.store("sp", "xc", [(seg_view(xcT_d, s * SEG, W), xc[:, :, :])], "xcT_d")
            pl = Pipe(P)
            for half in range(2):
                for b in range(10):
                    bk = ps[2 + pl.parity()]

                    def mm(e, half=half, b=b, bk=bk):
                        r = None
                        for q in range(4):
                            r = e.matmul(bk[:, q * 128:(q + 1) * 128], xc[:, b * 4 + q, half * 128:(half + 1) * 128], idn.t[:, :],
                                         start=True, stop=True)
                        return r
                    pl.pe(mm, waits=[tid])
                    eng = "act" if b % 2 == 0 else "dve"
                    if eng == "act":
                        pl.evac("act", lambda e, half=half, b=b, bk=bk: e.activation(out=xtk[:, half, b * 512:(b + 1) * 512], in_=bk[:, :], func=AF.Copy))
                    else:
                        pl.evac("dve", lambda e, half=half, b=b, bk=bk: e.tensor_copy(xtk[:, half, b * 512:(b + 1) * 512], bk[:, :]))
                    pl.next()
            pl.end()
            P.store("sp", "xtk", [(cview(xtok_d, s * 2, 2), xtk[:, :, :])], "xtok_d")
        P.barrier()
        tri = Slot(P, "tri", [128, 2, 128], BF16)
        trif = Slot(P, "trif", [128, 2, 128], F32)
        abc = Slot(P, "abc", [128, 128], F32)
        smk = Slot(P, "smk", [128, 32], F32)
        ttr = P.load("pool", tri, [(tri.t[:, :, :], tri_in.rearrange("two p l -> p two l"))])
        ttf = P.load("sp", trif, [(trif.t[:, :, :], tri_in.rearrange("two p l -> p two l"))])
        tal = P.load("sp", abc, [(abc.t[:, :], alog_bc)])
        tsm = P.load("sp", smk, [(smk.t[:, :], scanmask)])
        P.chain("act", lambda e: e.activation(out=abc.t[:, :], in_=abc.t[:, :], func=AF.Exp), waits=[tal])
        P.chain("dve", lambda e: e.tensor_scalar(abc.t[:, :], abc.t[:, :], -1.0, None, op0=ALU.mult))

        def mkbufs(dr):
            B = {}
            sfx = "_%d" % dr
            B["stS"] = Slot(P, "st" + sfx, [128, DI], F32)
            B["stbf"] = P.sb("stbf" + sfx, [128, DI], BF16)
            B["xks"] = [Slot(P, "xk%d%s" % (i, sfx), [128, 5120], BF16) for i in range(2)]
            B["dks"] = [Slot(P, "dk%d%s" % (i, sfx), [128, 128], F32) for i in range(2)]
            B["bcs"] = [Slot(P, "bcs%d%s" % (i, sfx), [128, 16, 128], BF16) for i in range(2)]
            for nm, shp, dt in (("dta", [128, 64], F32), ("dhi", [128, 64], BF16), ("dlo", [128, 64], BF16), ("t64", [128, 64], F32),
                                ("acT", [128, 64], F32), ("cbm", [128, 128], F32), ("segt", [128, 8, 128], F32), ("Lm", [128, 8, 128], F32),
                                ("Lmb", [128, 8, 128], BF16), ("Eo", [128, 8, 128], F32), ("Cs", [128, 8, 128], BF16),
                                ("xdt", [128, 8, 64], BF16), ("xdd", [128, 8, 64], BF16), ("w1", [128, 8], F32), ("w2", [128, 8], F32),
                                ("w3", [128, 8], F32), ("dec", [128, 8], F32), ("yst", [128, 32, 128], F32)):
                B[nm] = P.sb(nm + sfx, shp, dt)
            return B

        def scan_stream(dr, B):
            order = list(range(16)) if dr == 0 else list(range(15, -1, -1))
            last = 127 if dr == 0 else 0
            yd = yf_d if dr == 0 else yb_d
            ydn = "yf_d" if dr == 0 else "yb_d"
            pb = 4 * dr
            psA = psall[:, pb * 512:(pb + 2) * 512]
            psA3 = psA.rearrange("p (r l) -> p r l", l=128)
            alast = psall[:, pb * 512 + last:(pb + 2) * 512:128]
            bkc = ps[pb + 2]
            bky = ps[pb + 3]
            stS = B["stS"]
            st = stS.t
            stbf = B["stbf"]
            dta, dhi, dlo, t64, acT, cbm = B["dta"], B["dhi"], B["dlo"], B["t64"], B["acT"], B["cbm"]
            segt, Lm, Lmb, Eo, Cs, xdt, xdd = B["segt"], B["Lm"], B["Lmb"], B["Eo"], B["Cs"], B["xdt"], B["xdd"]
            w1, w2, w3, dec, yst = B["w1"], B["w2"], B["w3"], B["dec"], B["yst"]
            ystn = "yst_%d" % dr
            stn = "st_%d" % dr
            abd = abc.t[:, dr * 64:(dr + 1) * 64]
            trf = trif.t[:, dr, :]
            trm = tri.t[:, dr, :]
            th = P.load("sp", stS, [(st[:, :], h0T[dr])])
            P.chain("act", lambda e: e.activation(out=stbf[:, :], in_=st[:, :], func=AF.Copy), waits=[th])
            yield
            cl = 0
            for c in order:
                xk = B["xks"][cl % 2]
                dk = B["dks"][cl % 2]
                bcs = B["bcs"][cl % 2]
                cl += 1
                t1 = P.load("sp", xk, [(xk.t[:, :], xtok_d[c * 128:(c + 1) * 128, :])], dram_names=["xtok_d"])
                t2 = P.load("sp", dk, [(dk.t[:, :], dt_tok_d[c * 128:(c + 1) * 128, :])], dram_names=["dt_tok_d"])
                t3 = P.load("sp", bcs, [(bcs.t[:, :, :], xcT_d.rearrange("(k p) t -> p k t", p=128)[:, 32:48, c * 128:(c + 1) * 128])],
                            dram_names=["xcT_d"])
                dtd = dk.t[:, dr * 64:(dr + 1) * 64]
                P.chain("dve", lambda e, dtd=dtd: e.tensor_tensor(dta[:, :], dtd, abd, op=ALU.mult), waits=[t2])
                yield
                P.chain("dve", lambda e: e.tensor_copy(dhi[:, :], dta[:, :]))
                yield
                P.chain("dve", lambda e: e.tensor_tensor(t64[:, :], dta[:, :], dhi[:, :], op=ALU.subtract))
                yield
                P.chain("dve", lambda e: e.tensor_copy(dlo[:, :], t64[:, :]))
                yield

                def mm(e):
                    e.matmul(bkc[:, 128:192], trm, dhi[:, :], start=True, stop=False)
                    return e.matmul(bkc[:, 128:192], trm, dlo[:, :], start=False, stop=True)
                P.chain("pe", mm, waits=[ttr])
                yield
                P.chain("act", lambda e: e.activation(out=acT[:, :], in_=bkc[:, 128:192], func=AF.Copy))
                yield
                J = [P.last]
                for g in range(8):
                    hs0 = 8 * g
                    aT8 = acT[:, hs0:hs0 + 8]
                    dt8 = dk.t[:, dr * 64 + hs0:dr * 64 + hs0 + 8]
                    xg = xk.t[:, g * 512:(g + 1) * 512].rearrange("p (r q) -> p r q", q=64)
                    stg = st[:, g * 512:(g + 1) * 512]
                    stg3 = stg.rearrange("p (r q) -> p r q", q=64)

                    def mm(e, g=g, hs0=hs0, bcs=bcs):
                        e.matmul(bkc[:, 0:128], bcs.t[:, g, :], bcs.t[:, 8 + g, :], start=True, stop=True)
                        r = None
                        for r_ in range(8):
                            hh = hs0 + r_
                            e.matmul(psA[:, r_ * 128:(r_ + 1) * 128], dhi[:, hh:hh + 1].to_broadcast([128, 128]), trm, start=True, stop=False)
                            r = e.matmul(psA[:, r_ * 128:(r_ + 1) * 128], dlo[:, hh:hh + 1].to_broadcast([128, 128]), trm, start=False, stop=True)
                        return r
                    pe1 = P.chain("pe", mm, waits=[t3], deps=J)
                    yield

                    def d1f(e, aT8=aT8, xg=xg, dt8=dt8):
                        e.tensor_tensor(cbm[:, :], bkc[:, 0:128], trf, op=ALU.mult)
                        e.tensor_tensor(segt[:, :, :], psA3, aT8.unsqueeze(2).to_broadcast([128, 8, 128]), op=ALU.subtract)
                        e.tensor_tensor(xdt[:, :, :], xg, dt8.unsqueeze(2).to_broadcast([128, 8, 64]), op=ALU.mult)
                        return e.tensor_tensor(w1[:, :], alast, aT8, op=ALU.subtract)
                    d1 = P.chain("dve", d1f, waits=[ttf, t1], deps=[pe1])
                    yield

                    def a1f(e):
                        e.activation(out=Eo[:, :, :], in_=psA3, func=AF.Exp)
                        return e.activation(out=dec[:, :], in_=alast, func=AF.Exp)
                    a1 = P.chain("act", a1f, deps=[d1])
                    yield

                    def d2f(e, g=g, bcs=bcs):
                        e.tensor_scalar(segt[:, :, :], segt[:, :, :], 0.0, None, op0=ALU.min)
                        return e.tensor_tensor(Cs[:, :, :], Eo[:, :, :], bcs.t[:, 8 + g, :].unsqueeze(1).to_broadcast([128, 8, 128]), op=ALU.mult)
                    d2 = P.chain("dve", d2f, deps=[d1, a1])
                    yield

                    def a2f(e):
                        e.activation(out=Lm[:, :, :], in_=segt[:, :, :], func=AF.Exp)
                        return e.activation(out=w2[:, :], in_=w1[:, :], func=AF.Exp)
                    a2 = P.chain("act", a2f, deps=[d2])
                    yield

                    def d3f(e, dt8=dt8):
                        e.tensor_tensor(Lmb[:, :, :], Lm[:, :, :], cbm[:, :].unsqueeze(1).to_broadcast([128, 8, 128]), op=ALU.mult)
                        return e.tensor_tensor(w3[:, :], w2[:, :], dt8, op=ALU.mult)
                    d3 = P.chain("dve", d3f, deps=[a2])
                    yield
                    d4 = P.chain("dve", lambda e, xg=xg: e.tensor_tensor(xdd[:, :, :], xg, w3[:, :].unsqueeze(2).to_broadcast([128, 8, 64]), op=ALU.mult),
                                 deps=[d3])
                    yield

                    def mm(e, hs0=hs0):
                        r = None
                        for r_ in range(8):
                            pr, hf = r_ // 2, r_ % 2
                            out = bky[hf * 64:(hf + 1) * 64, pr * 128:(pr + 1) * 128]
                            hh = hs0 + r_
                            e.matmul(out, xdt[:, r_, :], Lmb[:, r_, :], start=True, stop=False)
                            r = e.matmul(out, stbf[:, hh * 64:(hh + 1) * 64], Cs[:, r_, :], start=False, stop=True)
                        return r
                    pe2 = P.chain("pe", mm, deps=[d3])
                    yield
                    a3 = P.chain("act", lambda e, g=g: e.activation(out=yst[:, 4 * g:4 * g + 4, :], in_=bky[:, :].rearrange("p (a l) -> p a l", l=128), func=AF.Copy),
                                 wbufs=[ystn] if g == 0 else [], deps=[pe2])
                    yield
                    pe3 = P.chain("pe", lambda e, g=g, xk=xk: e.matmul(bky[:, :], xk.t[:, DI + g * 128:DI + (g + 1) * 128],
                                                                        xdd[:, :, :].rearrange("p r q -> p (r q)"), start=True, stop=True),
                                  deps=[a3, d4])
                    yield
                    d5 = P.chain("dve", lambda e, stg3=stg3: e.tensor_tensor(stg3, stg3, dec[:, :].unsqueeze(2).to_broadcast([128, 8, 64]), op=ALU.mult),
                                 wbufs=[stn] if g == 0 else [], deps=[a1, d4])
                    yield
                    d6 = P.chain("dve", lambda e, stg=stg: e.tensor_tensor(stg, stg, bky[:, :], op=ALU.add), deps=[pe3, d5])
                    yield
                    a4 = P.chain("act", lambda e, g=g, stg=stg: e.activation(out=stbf[:, g * 512:(g + 1) * 512], in_=stg, func=AF.Copy), deps=[d6, pe2])
                    yield
                    J = [pe3, a4, d6]
                xk.release(P)
                dk.release(P)
                bcs.release(P)
                P.store_after("sp", ystn, [(yd.rearrange("(k p) t -> p k t", p=128)[:, :, c * 128:(c + 1) * 128], yst[:, :, :])], ydn, J)
                seg_end = (c % 2 == 1) if dr == 0 else (c % 2 == 0)
                if seg_end:
                    P.store_after("sp", stn, [(st_out[c // 2, dr], st[:, :])], "st_out", J)
                col = dr * 16 + c
                P.chain("dve", lambda e, col=col: e.tensor_scalar(st[:, :], st[:, :], smk.t[:, col:col + 1], None, op0=ALU.mult),
                        waits=[tsm], wbufs=[stn], deps=J)
                yield
                P.chain("act", lambda e: e.activation(out=stbf[:, :], in_=st[:, :], func=AF.Copy))
                yield
            stS.release(P)

        start = P.last
        bufs = [mkbufs(0), mkbufs(1)]
        gens = [scan_stream(0, bufs[0]), scan_stream(1, bufs[1])]
        lasts = [start, start]
        active = [True, True]
        while any(active):
            for dr in range(2):
                if active[dr]:
                    P.last = lasts[dr]
                    try:
                        next(gens[dr])
                    except StopIteration:
                        active[dr] = False
                    lasts[dr] = P.last
        P.barrier()
        yfS = Slot(P, "yfS", [128, 32, W], F32)
        ybS = Slot(P, "ybS", [128, 32, W], F32)
        xfS = Slot(P, "xfS", [128, 32, W], BF16)
        zzS = Slot(P, "zzS", [128, 32, W], BF16)
        sq4 = P.sb("sq4", [128, 32, W], BF16)
        ybf = sq4
        x4 = Slot(P, "xt", [128, KC, W], F32)
        o4 = P.sb("o4", [128, KC, W], F32)
        sqn = P.sb("sqn", [128, KC, W], BF16)
        rs4 = P.sb("rs4", [128, W], F32)
        ln4 = P.sb("ln4", [128, W], F32)
        dgw = Slot(P, "dgw", [128, 64], F32)
        wos = [Slot(P, "wo%d" % i, [128, 32, 128], BF16) for i in range(3)]
        tdg = P.load("sp", dgw, [(dgw.t[:, :], ssd_dgT)])
        yf3 = yfS.t[:, :, :]
        x3 = x4.t[:, :, :]
        co = 0
        for s in range(NSEG):
            ta = P.load("sp", yfS, [(yf3, seg_view(yf_d, s * SEG, W))], dram_names=["yf_d"])
            tb = P.load("sp", ybS, [(ybS.t[:, :, :], seg_view(yb_d, s * SEG, W))], dram_names=["yb_d"])
            tcx = P.load("sp", xfS, [(xfS.t[:, :, :], xcT_d.rearrange("(k p) t -> p k t", p=128)[:, 0:32, s * SEG:(s + 1) * SEG])], dram_names=["xcT_d"])
            tz = P.load("sp", zzS, [(zzS.t[:, :, :], seg_view(zs_d, s * SEG, W))], dram_names=["zs_d"])
            tx = P.load("sp", x4, [(x3, seg_view(xs, s * SEG, W))], dram_names=["xs"])
            P.chain("dve", lambda e: e.tensor_tensor(yf3, yf3, ybS.t[:, :, :], op=ALU.add), waits=[ta, tb])

            def dsk(e):
                r = None
                for fc in range(32):
                    r = e.scalar_tensor_tensor(out=yfS.t[:, fc, :], in0=xfS.t[:, fc, :], scalar=dgw.t[:, fc:fc + 1], in1=yfS.t[:, fc, :],
                                               op0=ALU.mult, op1=ALU.add)
                return r
            P.chain("dve", dsk, waits=[tcx, tdg])
            P.chain("dve", lambda e: e.tensor_tensor(yf3, yf3, zzS.t[:, :, :], op=ALU.mult), waits=[tz])
            rms_stats(yf3, sq4[:, :, :], rs4[:, :], ln4[:, :], DI)
            P.chain("dve", lambda e: e.tensor_tensor(yf3, yf3, bc_mid(rs4[:, :], 32), op=ALU.mult))

            def gsc(e):
                r = None
                for fc in range(32):
                    r = e.tensor_scalar(ybf[:, fc, :], yfS.t[:, fc, :], dgw.t[:, 32 + fc:33 + fc], None, op0=ALU.mult)
                return r
            P.chain("dve", gsc)
            ybS.release(P)
            xfS.release(P)
            zzS.release(P)
            yfS.release(P)
            pl = Pipe(P)
            for dc in range(KC):
                sl = wos[co % 3]
                co += 1
                tok = P.load("pool", sl, [(sl.t[:, :, :], ssd_out[dc].rearrange("p (k j) -> p k j", j=128))])
                bk = ps[3 + pl.parity()]

                def mm(e, sl=sl, bk=bk):
                    r = None
                    for k in range(32):
                        r = e.matmul(bk[:, 0:W], sl.t[:, k, :], ybf[:, k, :], start=(k == 0), stop=(k == 31))
                    return r
                pl.pe(mm, waits=[tok])
                sl.release(P)
                pl.evac("act", lambda e, dc=dc, bk=bk: e.activation(out=o4[:, dc, :], in_=bk[:, 0:W], func=AF.Copy))
                pl.next()
            pl.end()
            postnorm_add(x3, o4[:, :, :], sqn[:, :, :], rs4[:, :], ln4[:, :], li, j, waits=[tx])
            P.store("sp", "xt", [(seg_view(xs, s * SEG, W), x3)], "xs")
            x4.release(P)
        P.barrier()

    names = {"in": (xT_in, "xT"), "xs": (xs, "xs"), "out": (yT, "yT")}
    for st in stages:
        kind = st[0]
        if kind == "adaln":
            stage_adaln()
        elif kind == "pool":
            stage_pool(st[1])
        elif kind == "conv":
            stage_conv(st[1])
        elif kind == "ssd":
            stage_ssd(st[1])
        elif kind == "ffn":
            _, li, j, s, d = st
            stage_ffn2(li, j, names[s][0], names[d][0], names[s][1], names[d][1])
        else:
            raise ValueError(kind)
    P.final_wait()

    with nc.Block() as block:
        @block.tensor
        def _(e):
            for f in P.ops["pe"]:
                f(e)

        @block.scalar
        def _(e):
            for f in P.ops["act"]:
                f(e)

        @block.vector
        def _(e):
            for f in P.ops["dve"]:
                f(e)

        @block.sync
        def _(e):
            for f in P.ops["sp"]:
                f(e)

        @block.gpsimd
        def _(e):
            for f in P.ops["pool"]:
                f(e)
    es.close()
    return nc


def _pool_mats(prompt):
    def win1d(n, w):
        m = np.zeros((n, n), np.float64)
        for pos in range(n):
            lo = min(max(pos - w // 2, 0), n)
            hi = min(max(pos + (w - w // 2), 0), n)
            m[pos, lo:hi] = 1.0 / (hi - lo)
        return m
    out = np.zeros((4, T, T), np.float32)
    for g, w in enumerate((2, 4, 8, 16)):
        if prompt:
            m1 = win1d(256, w)
            M = np.kron(np.eye(8), m1)
        else:
            M = np.kron(win1d(32, w), win1d(64, w))
        M = M - np.eye(T)
        out[g] = M.T
    return out.astype(ml_dtypes.bfloat16)


def prep_shared(inp):
    f = np.float32
    sh = {}
    sh["ada_w"] = np.ascontiguousarray(inp["ada_w"], dtype=f)
    ab = np.asarray(inp["ada_b"], f).reshape(DEPTH, 9, KC, 128)
    sh["ada_bT"] = np.ascontiguousarray(ab.transpose(3, 0, 1, 2).reshape(128, -1))
    nw = np.asarray(inp["norm_w"], f).reshape(DEPTH, 3, 2, KC, 128)
    sh["nwT"] = np.ascontiguousarray(nw.transpose(4, 0, 1, 2, 3).reshape(128, -1))

    def colblk(w, nk):
        ncol = w.shape[1]
        return np.ascontiguousarray(w.reshape(nk, 128, ncol // 128, 128).transpose(2, 1, 0, 3).reshape(ncol // 128, 128, nk * 128))
    for nm, key in (("wg", "ffn_wg"), ("wu", "ffn_wu")):
        w = np.asarray(inp[key], f).reshape(8, D, FF)
        sh[nm] = np.stack([colblk(w[i], KC) for i in range(8)])
    w = np.asarray(inp["ffn_wd"], f).reshape(8, FF, D)
    sh["wd"] = np.stack([colblk(w[i], FC) for i in range(8)])
    pw = np.asarray(inp["pool_w"], f).reshape(2, 4, 4, 128, 512)
    sh["pool_wT"] = np.ascontiguousarray(pw.transpose(0, 3, 1, 2, 4).reshape(2, 128, -1))
    psc = np.asarray(inp["pool_scale"], f).reshape(2, KC, 128)
    sh["pool_scT"] = np.ascontiguousarray(psc.transpose(2, 0, 1).reshape(128, -1))
    p1 = colblk(np.asarray(inp["cm_pw1"], f)[0], KC)
    sh["pw1"] = np.ascontiguousarray(np.stack([p1[:KC], p1[KC:]], axis=1))
    dw = np.asarray(inp["cm_dw_w"], f)[0]
    dg = np.zeros((KC, 128, 31, 128), f)
    ar = np.arange(128)
    for c in range(KC):
        dg[c, ar, :, ar] = dw[:, c * 128:(c + 1) * 128].T
    sh["cm_diag"] = dg.reshape(KC, 128, 31 * 128)
    vec = np.stack([np.asarray(inp[k], f)[0] for k in ("cm_dw_b", "cm_ln_w", "cm_ln_b")])
    sh["cm_vecT"] = np.ascontiguousarray(vec.reshape(3, KC, 128).transpose(2, 0, 1).reshape(128, -1))
    sh["pw2"] = colblk(np.asarray(inp["cm_pw2"], f)[0], KC)
    inw = np.asarray(inp["ssd_in_w"], f)[0]
    sh["ssd_in"] = colblk(inw[:, :10240], KC)
    sh["ssd_wdt"] = np.ascontiguousarray(inw[:, 10240:].reshape(KC, 128, 128).transpose(1, 0, 2).reshape(128, -1))
    cw = np.asarray(inp["ssd_conv_w"], f)[0]
    dg = np.zeros((48, 128, 5, 128), f)
    for c in range(48):
        dg[c, ar, :, ar] = cw[:, c * 128:(c + 1) * 128].T
    sh["ssd_diag"] = dg.reshape(48, 128, 5 * 128)
    sh["ssd_cbT"] = np.ascontiguousarray(np.asarray(inp["ssd_conv_b"], f)[0].reshape(48, 128).T)
    sh["dtb_bc"] = np.ascontiguousarray(np.broadcast_to(np.asarray(inp["ssd_dt_bias"], f)[0].reshape(1, 128), (128, 128)))
    sh["alog_bc"] = np.ascontiguousarray(np.broadcast_to(np.asarray(inp["ssd_a_log"], f)[0].reshape(1, 128), (128, 128)))
    dfeat = np.repeat(np.asarray(inp["ssd_d"], f)[0], 64)
    gw = np.asarray(inp["ssd_norm_w"], f)[0]
    sh["ssd_dgT"] = np.ascontiguousarray(np.concatenate([dfeat.reshape(32, 128).T, gw.reshape(32, 128).T], axis=1))
    sh["ssd_out"] = colblk(np.asarray(inp["ssd_out_w"], f)[0], 32)
    sh["ident"] = np.eye(128, dtype=f)
    tu = np.triu(np.ones((128, 128), f))
    sh["tri_in"] = np.ascontiguousarray(np.stack([tu, tu.T]))
    return sh


_POOLP = {}


def core_inputs(inp, core):
    f = np.float32
    m = {}
    prompt = core < 4
    if prompt:
        x = np.asarray(inp["x_prompt"], f)[core * 8:(core + 1) * 8].reshape(T, D)
        c = np.asarray(inp["c_ctx"], f)
        h0 = np.zeros((2, 128, 4096), f)
    else:
        x = np.asarray(inp["x_sample"], f)[core - 4].reshape(T, D)
        c = np.asarray(inp["c"], f)[core - 4]
        s0 = np.asarray(inp["state_ssd"], f)[core - 4, 0]
        h0 = np.ascontiguousarray(s0.reshape(2, 4096, 128).transpose(0, 2, 1))
    m["xT"] = np.ascontiguousarray(x.T)
    m["cvec"] = np.ascontiguousarray(c.reshape(KC, 128).T)
    m["h0T"] = h0
    m["cmask"] = np.full((128, 1), 0.0 if prompt else 1.0, f)
    sm = np.ones((2, 16), f)
    if prompt:
        sm[0, 1::2] = 0.0
        sm[1, 0::2] = 0.0
    m["scanmask"] = np.ascontiguousarray(np.broadcast_to(sm.reshape(1, 32), (128, 32)))
    if prompt not in _POOLP:
        _POOLP[prompt] = _pool_mats(prompt)
    m["poolP"] = _POOLP[prompt]
    return m


def full_stages():
    st = [("adaln",)]
    for li in range(DEPTH):
        st.append(("ffn", li, 0, "in" if li == 0 else "xs", "xs"))
        st.append((("pool", "ssd", "conv")[li % 3], li))
        st.append(("ffn", li, 2, "xs", "out" if li == DEPTH - 1 else "xs"))
    return st


def kernel(**inp):
    sh = prep_shared(inp)
    in_maps = []
    for core in range(NCORES):
        m = dict(sh)
        m.update(core_inputs(inp, core))
        in_maps.append(m)
    nc = build_program(full_stages())
    res = run_bass_kernel_spmd(nc, in_maps, core_ids=list(range(NCORES)))
    r = res.results
    yp = np.stack([r[c]["yT"].T for c in range(4)]).reshape(32, 256, D)
    ys = np.stack([r[c]["yT"].T for c in range(4, 8)]).reshape(4, T, D)
    so = np.stack([r[c]["st_out"] for c in range(4)]).reshape(32, 2, 128, 64, 64)
    ns = np.ascontiguousarray(so.transpose(0, 1, 3, 4, 2)).reshape(32, 1, 2, 64, 64, 128)
    return (np.ascontiguousarray(yp, dtype=np.float32), np.ascontiguousarray(ys, dtype=np.float32),
            np.ascontiguousarray(ns, dtype=np.float32))
```
